# Optimizing a Trainium2 kernel written in Bass

```python
import jax, jax.numpy as jnp
from jax import lax
import numpy as np

D_MODEL = 2048
BATCH = 8
SEQ = 2048
DEPTH = 4
DEC_BATCH = 32
DEC_SEQ = 32
PAST_LEN = 1024

CHUNK = 64
D_A = D_MODEL // 2
D_B = D_MODEL // 2
CONV_W = 3
GMLP_CHUNK = 128
GMLP_GROUPS = 8
GMLP_GD = D_B // GMLP_GROUPS
SB_HEADS = 16
SB_HEAD_DIM = D_MODEL // SB_HEADS
QBLOCK = 128
D_FF = ((8 * D_MODEL // 3 + 255) // 256) * 256
EPS = 1e-6

kernel_name = 'hybrid_streaming_encoder_step'


def rms_norm(x, g):
    x32 = x.astype(jnp.float32)
    y = x32 * lax.rsqrt(jnp.mean(x32 * x32, axis=-1, keepdims=True) + EPS)
    return y.astype(x.dtype) * g


def causal_dwconv(x, w, prev):
    T = x.shape[1]
    xp = jnp.concatenate([prev, x], axis=1)
    y = xp[:, 0:T] * w[0]
    for k in range(1, CONV_W):
        y = y + xp[:, k:k + T] * w[k]
    return y, xp[:, -(CONV_W - 1):]


def chunk_sgu(u, v, w_s, b_s):
    B, T, _ = v.shape
    Tp = -(-T // GMLP_CHUNK) * GMLP_CHUNK
    vp = jnp.pad(v, ((0, 0), (0, Tp - T), (0, 0)))
    vp = vp.reshape(B, Tp // GMLP_CHUNK, GMLP_CHUNK, GMLP_GROUPS, GMLP_GD)
    blk = jnp.arange(GMLP_CHUNK) // CHUNK
    mask = blk[None, :] <= blk[:, None]
    ws = jnp.where(mask[None], w_s, 0.0)
    mixed = jnp.einsum('gij,bcjgd->bcigd', ws, vp) + b_s.T[None, None, :, :, None]
    mixed = mixed.reshape(B, Tp, D_B)[:, :T]
    return u * mixed


def stick_breaking_attention(q, k, v, q_offset):
    B, Tq, H, Dh = q.shape
    Tk = k.shape[1]
    qb = QBLOCK if Tq % QBLOCK == 0 else Tq
    nb = Tq // qb
    scale = Dh ** -0.5
    k32 = k.astype(jnp.float32)
    kpos = jnp.arange(Tk)

    def block(args):
        q_blk, qpos = args
        z = jnp.einsum('bqhd,bkhd->bhqk', q_blk.astype(jnp.float32), k32) * scale
        mask = kpos[None, :] < qpos[:, None]
        log_fail = jnp.where(mask, jax.nn.log_sigmoid(-z), 0.0)
        later = lax.cumsum(log_fail, axis=3, reverse=True) - log_fail
        w = jnp.where(mask, jnp.exp(jax.nn.log_sigmoid(z) + later), 0.0)
        return jnp.einsum('bhqk,bkhd->bqhd', w.astype(v.dtype), v)

    q_blocks = q.reshape(B, nb, qb, H, Dh).transpose(1, 0, 2, 3, 4)
    qpos = (q_offset + jnp.arange(Tq)).reshape(nb, qb)
    out = lax.map(block, (q_blocks, qpos))
    return out.transpose(1, 0, 2, 3, 4).reshape(B, Tq, H, Dh)


def conv_chunk_mixer(h, w_in, w_conv, g_sgu, w_s, b_s, w_out, conv_prev):
    z = h @ w_in
    xa, gb, gc, u, v = jnp.split(z, [D_A, 2 * D_A, 3 * D_A, 3 * D_A + D_B], axis=-1)
    conv_out, conv_state = causal_dwconv(gc * xa, w_conv, conv_prev)
    ya = gb * conv_out
    vn = rms_norm(v, g_sgu)
    yb = chunk_sgu(u, vn, w_s, b_s)
    y = jnp.concatenate([ya, yb], axis=-1) @ w_out
    return y, conv_state, vn


def sb_mixer(h, w_qkv, w_o, k_past, v_past, q_offset):
    B, T, _ = h.shape
    q, k, v = jnp.split(h @ w_qkv, 3, axis=-1)
    q = q.reshape(B, T, SB_HEADS, SB_HEAD_DIM)
    k = k.reshape(B, T, SB_HEADS, SB_HEAD_DIM)
    v = v.reshape(B, T, SB_HEADS, SB_HEAD_DIM)
    if k_past is None:
        kk, vv = k, v
    else:
        kk = jnp.concatenate([k_past, k], axis=1)
        vv = jnp.concatenate([v_past, v], axis=1)
    o = stick_breaking_attention(q, kk, vv, q_offset)
    return o.reshape(B, T, D_MODEL) @ w_o, k, v


def conv_ffn(h, w_up, w_conv, b_conv, w_down, prev):
    a, g = jnp.split(h @ w_up, 2, axis=-1)
    a, new_prev = causal_dwconv(a, w_conv, prev)
    y = (jax.nn.silu(a + b_conv) * g) @ w_down
    return y, new_prev


def trunk(x, c, k_past, v_past, conv_a_prev, ffn_prev, w_mod, b_mod, norm_g, w_in_ab, w_conv_a,
          g_sgu, w_sgu, b_sgu, w_out_ab, w_qkv_sb, w_o_sb, w_ffn_up, w_ffn_conv, b_ffn_conv, w_ffn_down):
    q_offset = 0 if k_past is None else k_past.shape[2]
    k_new, v_new, conv_a_new, ffn_new, sgu_v_new = [], [], [], [], []
    for l in range(DEPTH):
        i = l // 2
        mod = jax.nn.silu(c) @ w_mod[l] + b_mod[l]
        sh1, sc1, g1, sh2, sc2, g2 = jnp.split(mod[:, None, :], 6, axis=-1)
        h = rms_norm(x, norm_g[l, 0]) * (1 + sc1) + sh1
        if l % 2 == 0:
            o, cs, vr = conv_chunk_mixer(h, w_in_ab[i], w_conv_a[i], g_sgu[i], w_sgu[i], b_sgu[i],
                                         w_out_ab[i], conv_a_prev[i])
            conv_a_new.append(cs)
            sgu_v_new.append(vr)
        else:
            kp = None if k_past is None else k_past[i]
            vp = None if v_past is None else v_past[i]
            o, kn, vn = sb_mixer(h, w_qkv_sb[i], w_o_sb[i], kp, vp, q_offset)
            k_new.append(kn)
            v_new.append(vn)
        x = x + g1 * rms_norm(o, norm_g[l, 1])
        h = rms_norm(x, norm_g[l, 2]) * (1 + sc2) + sh2
        o, fs = conv_ffn(h, w_ffn_up[l], w_ffn_conv[l], b_ffn_conv[l], w_ffn_down[l], ffn_prev[l])
        ffn_new.append(fs)
        x = x + g2 * rms_norm(o, norm_g[l, 3])
    return x, jnp.stack(k_new), jnp.stack(v_new), jnp.stack(conv_a_new), jnp.stack(ffn_new), sgu_v_new


def setup_inputs(seed: int = 0) -> dict:
    key = jax.random.key(seed)
    ks = jax.random.split(key, 24)
    n_even = (DEPTH + 1) // 2
    n_odd = DEPTH // 2
    d_in = 3 * D_A + 2 * D_B

    def nrm(k, shape, scale):
        return jax.random.normal(k, shape, jnp.float32) * scale

    return {
        'x_prompt': nrm(ks[0], (BATCH, SEQ, D_MODEL), 1.0),
        'x_sample': nrm(ks[1], (DEC_BATCH, DEC_SEQ, D_MODEL), 1.0),
        'c_prompt': nrm(ks[2], (BATCH, D_MODEL), 1.0),
        'c_sample': nrm(ks[3], (DEC_BATCH, D_MODEL), 1.0),
        'cache_sb_k': nrm(ks[4], (n_odd, DEC_BATCH, PAST_LEN, SB_HEADS, SB_HEAD_DIM), 1.0),
        'cache_sb_v': nrm(ks[5], (n_odd, DEC_BATCH, PAST_LEN, SB_HEADS, SB_HEAD_DIM), 1.0),
        'state_conv_a': nrm(ks[6], (n_even, DEC_BATCH, CONV_W - 1, D_A), 1.0),
        'state_ffn_conv': nrm(ks[7], (DEPTH, DEC_BATCH, CONV_W - 1, D_FF), 1.0),
        'w_mod': nrm(ks[8], (DEPTH, D_MODEL, 6 * D_MODEL), 0.5 * D_MODEL ** -0.5),
        'b_mod': nrm(ks[9], (DEPTH, 6 * D_MODEL), 0.02),
        'norm_g': 1.0 + nrm(ks[10], (DEPTH, 4, D_MODEL), 0.02),
        'w_in_ab': nrm(ks[11], (n_even, D_MODEL, d_in), D_MODEL ** -0.5),
        'w_conv_a': nrm(ks[12], (n_even, CONV_W, D_A), CONV_W ** -0.5),
        'g_sgu': 1.0 + nrm(ks[13], (n_even, D_B), 0.02),
        'w_sgu': nrm(ks[14], (n_even, GMLP_GROUPS, GMLP_CHUNK, GMLP_CHUNK), GMLP_CHUNK ** -0.5),
        'b_sgu': 1.0 + nrm(ks[15], (n_even, GMLP_GROUPS, GMLP_CHUNK), 0.1),
        'w_out_ab': nrm(ks[16], (n_even, D_A + D_B, D_MODEL), (D_A + D_B) ** -0.5),
        'w_qkv_sb': nrm(ks[17], (n_odd, D_MODEL, 3 * D_MODEL), D_MODEL ** -0.5),
        'w_o_sb': nrm(ks[18], (n_odd, D_MODEL, D_MODEL), D_MODEL ** -0.5),
        'w_ffn_up': nrm(ks[19], (DEPTH, D_MODEL, 2 * D_FF), D_MODEL ** -0.5),
        'w_ffn_conv': nrm(ks[20], (DEPTH, CONV_W, D_FF), CONV_W ** -0.5),
        'b_ffn_conv': nrm(ks[21], (DEPTH, D_FF), 0.02),
        'w_ffn_down': nrm(ks[22], (DEPTH, D_FF, D_MODEL), D_FF ** -0.5),
    }


def reference(x_prompt, x_sample, c_prompt, c_sample, cache_sb_k, cache_sb_v, state_conv_a, state_ffn_conv,
              w_mod, b_mod, norm_g, w_in_ab, w_conv_a, g_sgu, w_sgu, b_sgu, w_out_ab, w_qkv_sb, w_o_sb,
              w_ffn_up, w_ffn_conv, b_ffn_conv, w_ffn_down):
    n_even = (DEPTH + 1) // 2
    bp = x_prompt.shape[0]
    zeros_a = jnp.zeros((n_even, bp, CONV_W - 1, D_A), x_prompt.dtype)
    zeros_f = jnp.zeros((DEPTH, bp, CONV_W - 1, D_FF), x_prompt.dtype)
    y_prompt, k_p, v_p, conv_a_p, ffn_p, _ = trunk(
        x_prompt, c_prompt, None, None, zeros_a, zeros_f,
        w_mod, b_mod, norm_g, w_in_ab, w_conv_a, g_sgu, w_sgu, b_sgu, w_out_ab, w_qkv_sb, w_o_sb,
        w_ffn_up, w_ffn_conv, b_ffn_conv, w_ffn_down)
    y_sample, k_s, v_s, conv_a_s, ffn_s, sgu_v_list = trunk(
        x_sample, c_sample, cache_sb_k, cache_sb_v, state_conv_a, state_ffn_conv,
        w_mod, b_mod, norm_g, w_in_ab, w_conv_a, g_sgu, w_sgu, b_sgu, w_out_ab, w_qkv_sb, w_o_sb,
        w_ffn_up, w_ffn_conv, b_ffn_conv, w_ffn_down)
    sgu_v_s = jnp.stack(sgu_v_list)
    return (y_prompt, y_sample, k_p, v_p, conv_a_p, ffn_p, k_s, v_s, conv_a_s, ffn_s, sgu_v_s)
```

```python
import numpy as np
import concourse.bass as bass
import concourse.mybir as mybir
from concourse.bass_utils import run_bass_kernel_spmd

F32 = mybir.dt.float32
BF16 = mybir.dt.bfloat16
ALU = mybir.AluOpType
AF = mybir.ActivationFunctionType

D = 2048
TP = 2048
TS = 128
T = TP + TS
NSEQ = 5
DFF = 5632
NFC = DFF // 128
DEPTH = 4
EPS = 1e-6
TILES = [(0, 512), (512, 512), (1024, 512), (1536, 512), (2048, 128)]
ENG = ['pe', 'act', 'dve', 'pool', 'sp']


class DmaSlot:
    def __init__(self, sem):
        self.sem = sem
        self.count = 0
        self.last = None


class Op:
    __slots__ = ('eng', 'idx', 'fn', 'dma', 'dma_val', 'waits', 'signal', 'clk', 'sig_no')

    def __init__(self, eng, idx, fn, dma):
        self.eng = eng
        self.idx = idx
        self.fn = fn
        self.dma = dma
        self.dma_val = 0
        self.waits = []
        self.signal = False
        self.clk = None
        self.sig_no = 0


class Sched:
    def __init__(self):
        self.ops = {e: [] for e in ENG}
        self.lastw = {}
        self.readers = {}
        self.clock = {e: [-1] * 5 for e in ENG}
        self.dmaw = {e: {} for e in ENG}
        self.pending = {e: [] for e in ENG}

    def add(self, eng, fn, reads=(), writes=(), dma=None):
        deps = []
        for k in reads:
            w = self.lastw.get(k)
            if w is not None:
                deps.append((w, True))
        for k in writes:
            w = self.lastw.get(k)
            if w is not None:
                deps.append((w, False))
            for r in self.readers.get(k, ()):
                deps.append((r, False))
        for p in self.pending[eng]:
            deps.append((p, True))
        self.pending[eng] = []
        op = Op(eng, len(self.ops[eng]), fn, dma)
        if dma is not None:
            if dma.last is not None:
                deps.append((dma.last, True))
            dma.count += 16
            op.dma_val = dma.count
            dma.last = op
        ei = ENG.index(eng)
        clk = self.clock[eng]
        for d, raw in deps:
            if d is op:
                continue
            if d.dma is not None:
                cur = self.dmaw[eng].get(d.dma, 0)
                if d.dma_val > cur:
                    self.dmaw[eng][d.dma] = d.dma_val
                    op.waits.append(d)
            else:
                di = ENG.index(d.eng)
                if d.idx > clk[di]:
                    if d.eng == eng and (eng == 'pe' or (not raw and eng != 'pool')):
                        continue
                    op.waits.append(d)
                    d.signal = True
                    dc = d.clk
                    for j in range(5):
                        if dc[j] > clk[j]:
                            clk[j] = dc[j]
        c = list(clk)
        c[ei] = op.idx
        op.clk = c
        self.ops[eng].append(op)
        for k in reads:
            self.readers.setdefault(k, []).append(op)
        for k in writes:
            self.lastw[k] = op
            self.readers[k] = []
        return op

    def fence(self):
        lasts = []
        for e in ENG:
            if self.ops[e]:
                if e == 'sp':
                    continue
                lasts.append(self.ops[e][-1])
        best = {}
        for k, w in self.lastw.items():
            if w.dma is not None:
                c = best.get(id(w.dma))
                if c is None or w.dma_val > c.dma_val:
                    best[id(w.dma)] = w
        for k, rs in self.readers.items():
            for r in rs:
                if r.dma is not None:
                    c = best.get(id(r.dma))
                    if c is None or r.dma_val > c.dma_val:
                        best[id(r.dma)] = r
        lasts.extend(best.values())
        for e in ENG:
            self.pending[e] = list(lasts)
        self.lastw = {}
        self.readers = {}

    def emit(self, nc, block, sems, final_waits):
        for e in ENG:
            n = 0
            for op in self.ops[e]:
                if op.signal and op.dma is None:
                    n += 1
                    op.sig_no = n

        def body_for(e):
            def body(eng):
                for op in self.ops[e]:
                    for d in op.waits:
                        if d.dma is not None:
                            eng.wait_ge(d.dma.sem, d.dma_val)
                        else:
                            eng.wait_ge(sems[d.eng], d.sig_no)
                    ins = op.fn(eng)
                    if op.dma is not None:
                        ins.then_inc(op.dma.sem, 16)
                    elif op.signal:
                        ins.then_inc(sems[e], 1)
                if e == 'sp':
                    for slot in final_waits:
                        if slot.count > 0:
                            eng.wait_ge(slot.sem, slot.count)
            return body
        block.tensor(body_for('pe'))
        block.scalar(body_for('act'))
        block.vector(body_for('dve'))
        block.gpsimd(body_for('pool'))
        block.sync(body_for('sp'))


class Builder:
    def __init__(self, nlayers=DEPTH, debug=False):
        self.nl = nlayers
        self.debug = debug
        self.nc = bass.Bass("TRN2", target_bir_lowering=False)
        self.s = Sched()
        self.slots = []
        self.psn = 0
        self.wcnt = 0
        self.pooln = {}
        import os as _os
        self.odd_mode = _os.environ.get('ODD_MODE', 'full')
        self.uid = 0

    def din(self, name, shape):
        return self.nc.dram_tensor(name, list(shape), F32, kind="ExternalInput").ap()

    def dout(self, name, shape):
        return self.nc.dram_tensor(name, list(shape), F32, kind="ExternalOutput").ap()

    def slot(self, name):
        sem = self.stack.enter_context(self.nc.semaphore(name))
        sl = DmaSlot(sem)
        self.slots.append(sl)
        return sl

    def sb(self, name, shape, dt):
        return self.stack.enter_context(self.nc.sbuf_tensor(name, list(shape), dt))

    def key(self, p):
        self.uid += 1
        return (p, self.uid)

    def bank(self, pool=None):
        if pool is None:
            b = self.psn % 8
            self.psn += 1
            return b
        n = self.pooln.get(tuple(pool), 0)
        self.pooln[tuple(pool)] = n + 1
        return pool[n % len(pool)]

    def dma(self, q, slot, out, in_, reads=(), writes=(), nc_ok=False):
        if nc_ok:
            fn = lambda e: e.dma_start(out=out, in_=in_, allow_slow_non_contiguous=True)
        else:
            fn = lambda e: e.dma_start(out=out, in_=in_)
        return self.s.add(q, fn, reads, writes, dma=slot)

    def mm(self, out, lhsT, rhs, start, stop, reads, writes):
        return self.s.add('pe', lambda e: e.matmul(out, lhsT=lhsT, rhs=rhs, start=start, stop=stop, skip_group_check=True),
                          reads, writes)

    def tr(self, out, in_, ident, reads, writes):
        return self.s.add('pe', lambda e: e.transpose(out, in_, ident), reads, writes)

    def act(self, out, in_, func, reads, writes, bias=None, scale=None, accum_out=None):
        kw = {}
        if bias is not None:
            kw['bias'] = bias
        if scale is not None:
            kw['scale'] = scale
        if accum_out is not None:
            kw['accum_out'] = accum_out
        return self.s.add('act', lambda e: e.activation(out, in_, func, **kw), reads, writes)

    def tt(self, eng, out, in0, in1, op, reads, writes):
        return self.s.add(eng, lambda e: e.tensor_tensor(out, in0, in1, op), reads, writes)

    def ts(self, eng, out, in0, s1, s2, op0, op1, reads, writes):
        if op1 is None:
            return self.s.add(eng, lambda e: e.tensor_scalar(out, in0, s1, None, op0), reads, writes)
        return self.s.add(eng, lambda e: e.tensor_scalar(out, in0, s1, s2, op0, op1), reads, writes)

    def stt(self, eng, out, in0, scalar, in1, op0, op1, reads, writes):
        return self.s.add(eng, lambda e: e.scalar_tensor_tensor(out, in0, scalar, in1, op0, op1), reads, writes)

    def copy(self, eng, out, in_, reads, writes):
        if eng == 'act':
            return self.s.add('act', lambda e: e.copy(out, in_), reads, writes)
        return self.s.add(eng, lambda e: e.tensor_copy(out, in_), reads, writes)

    def memset(self, eng, ap, val, writes):
        return self.s.add(eng, lambda e: e.memset(ap, val), (), writes)

    def build(self):
        from contextlib import ExitStack
        nc = self.nc
        with ExitStack() as stack:
            self.stack = stack
            self.declare()
            self.program()
            sems = {e: stack.enter_context(nc.semaphore("sem_" + e)) for e in ['pe', 'act', 'dve', 'pool']}
            block = stack.enter_context(nc.Block())
            self.s.emit(nc, block, sems, self.slots)
        return nc

    def declare(self):
        nc = self.nc
        i = self.din
        self.xp = i("xp", [TP, D])
        self.xs = i("xs", [TS, D])
        self.cc = i("cc", [NSEQ, D])
        self.ck = i("ck", [2, 4, 1024, D])
        self.cv = i("cv", [2, 4, 1024, D])
        self.sca = i("sca", [2, 4, 2, 1024])
        self.sfc = i("sfc", [4, 4, 2, DFF])
        self.w_mod = i("w_mod", [4, D, 6 * D])
        self.b_mod = i("b_mod", [4, 6 * D])
        self.norm_g = i("norm_g", [4, 4, D])
        self.w_in = i("w_in_ab", [2, D, 5120])
        self.w_conv_a = i("w_conv_a", [2, 3, 1024])
        self.g_sgu = i("g_sgu", [2, 1024])
        self.w_sgu = i("w_sgu", [2, 8, 128, 128])
        self.b_sgu = i("b_sgu", [2, 8, 128])
        self.w_out = i("w_out_ab", [2, D, D])
        self.w_qkv = i("w_qkv_sb", [2, D, 3 * D])
        self.w_o = i("w_o_sb", [2, D, D])
        self.w_up = i("w_ffn_up", [4, D, 2 * DFF])
        self.w_fconv = i("w_ffn_conv", [4, 3, DFF])
        self.b_fconv = i("b_ffn_conv", [4, DFF])
        self.w_down = i("w_ffn_down", [4, DFF, D])
        o = self.dout
        self.yp = o("yp", [TP, D])
        self.ys = o("ys", [TS, D])
        self.kp = o("kp", [2, TP, D])
        self.vp = o("vp", [2, TP, D])
        self.cap = o("cap", [2, 2, 1024])
        self.fcp = o("fcp", [4, 2, DFF])
        self.ks = o("ks", [2, TS, D])
        self.vs = o("vs", [2, TS, D])
        self.cas = o("cas", [2, 4, 2, 1024])
        self.fcs = o("fcs", [4, 4, 2, DFF])
        self.sgv = o("sgv", [2, TS, 1024])
        self.X = nc.dram_tensor("Xs", [16, 128, T], F32).ap()
        self.Y = nc.dram_tensor("Ys", [16, 128, T], F32).ap()
        self.QS = nc.dram_tensor("QSs", [16, 128, T], BF16).ap()
        self.KS = nc.dram_tensor("KSs", [16, 128, T], BF16).ap()
        if self.debug:
            self.dbgH = nc.dram_tensor("dbgH", [16, 128, T], BF16, kind="ExternalOutput").ap()
            self.dbgX = nc.dram_tensor("dbgX", [16, 128, T], F32, kind="ExternalOutput").ap()
            self.dbgM = nc.dram_tensor("dbgM", [128, 6 * 16 * NSEQ], F32, kind="ExternalOutput").ap()
        self.H = self.sb("H", [128, 16 * T], BF16)
        self.H3 = self.H[:, :].rearrange("p (c t) -> p c t", c=16)
        self.R = self.sb("R", [128, 47872], BF16)
        self.WS = self.sb("WS", [128, 4 * 2816], BF16)
        self.TMP = self.sb("TMP", [128, 4096], F32)
        self.ps = [self.stack.enter_context(nc.psum_tensor("ps%d" % b, [128, 512], F32)) for b in range(8)]
        self.ones_bf = self.sb("ones_bf", [128, 128], BF16)
        self.ident_f = self.sb("ident_f", [128, 128], F32)
        self.ident_b = self.sb("ident_b", [128, 128], BF16)
        self.ustrict = self.sb("ustrict", [128, 128], BF16)
        self.modT = self.sb("modT", [128, 96 * NSEQ], F32)
        self.epsD = self.sb("epsD", [128, 4], F32)
        self.gT = self.sb("gT", [128, 16 * 16], F32)
        self.vecs2 = [self.sb("vecs%d" % k, [128, 6 * 16 * NSEQ], F32) for k in range(2)]
        self.MODS = nc.dram_tensor("MODSs", [4, 128, 96 * NSEQ], F32).ap()
        Rf = self.R.bitcast(F32)
        self.iota_t = Rf[:, 0:512]
        self.bmT = Rf[:, 512:512 + 384]
        self.cT = Rf[:, 1024:1024 + 80]
        self.scT = self.R[:, 4096:4096 + 80]
        self.wl = [self.slot("wl%d" % k) for k in range(4)]
        self.sp_ld = [self.slot("ld%d" % k) for k in range(6)]
        self.sp_st = [self.slot("st%d" % k) for k in range(6)]
        self.misc = [self.slot("mi%d" % k) for k in range(4)]
        self.pl = [self.slot("pl%d" % k) for k in range(4)]

    def ws_view(self, sub, kc):
        off = sub * 2816
        return self.WS[:, off:off + kc * 128].rearrange("p (k n) -> p k n", k=kc)

    def program(self):
        self.setup_consts()
        self.mod_all()
        self.prologue()
        for l in range(self.nl):
            if l % 2 == 0:
                self.mixer_even(l)
            else:
                self.mixer_odd(l)
            self.update_pass(l, 0)
            self.ffn(l)
            self.update_pass(l, 1)
        if self.debug:
            self.s.fence()
            self.dma('sp', self.misc[0], self.dbgH.rearrange("c p t -> p c t"), self.H3, (), ())
            self.dma('sp', self.misc[1], self.dbgM, self.cur_vecs[:, :], (), ())
            self.dma('sp', self.misc[2], self.dbgX, self.X, (), ())

    def setup_consts(self):
        s = self.s
        K = 'const'
        self.memset('pool', self.ones_bf[:, :], 1.0, [K])
        self.memset('pool', self.epsD[:, 0:1], float(D * EPS), [K])
        self.memset('pool', self.epsD[:, 1:2], float(1024 * EPS), [K])
        def asel(out, in_, pattern, op, base, cm, rk, wk):
            return self.s.add('pool', lambda e: e.affine_select(out, in_, pattern, op, 0.0, base=base,
                                                                channel_multiplier=cm), rk, wk)
        self.asel = asel
        asel(self.ident_f[:, :], self.ones_bf[:, :], [[1, 128]], ALU.is_equal, 0, -1, [K], ['id'])
        asel(self.ident_b[:, :], self.ones_bf[:, :], [[1, 128]], ALU.is_equal, 0, -1, [K], ['id'])
        asel(self.ustrict[:, :], self.ones_bf[:, :], [[-1, 128]], ALU.is_gt, 0, 1, [K], ['id'])
        for a16 in range(16):
            self.dma('sp', self.misc[0], self.gT[:, a16 * 16:(a16 + 1) * 16],
                     self.norm_g[a16 // 4, a16 % 4, :].rearrange("(c p) -> p c", p=128), (), ['gT'], nc_ok=True)
        for l in range(4):
            self.dma('sp', self.misc[1], self.bmT[:, l * 96:(l + 1) * 96],
                     self.b_mod[l, :].rearrange("(n p) -> p n", p=128), (), ['bmT'], nc_ok=True)
        cT3 = self.cT.rearrange("p (c q) -> p c q", q=NSEQ)
        for q in range(NSEQ):
            self.dma('sp', self.misc[2], cT3[:, :, q], self.cc[q, :].rearrange("(c p) -> p c", p=128), (), ['cT'],
                     nc_ok=True)
        self.act(self.scT, self.cT, AF.Silu, ['cT'], ['scT'])

    def mod_all(self):
        cnt = 0
        scT3 = self.scT.rearrange("p (c q) -> p c q", q=NSEQ)
        for l in range(min(self.nl + 1, 4)):
            for n2 in range(48):
                big = cnt % 2
                cnt += 1
                wv = self.WS[:, big * 5632:big * 5632 + 4096].rearrange("p (k n) -> p k n", k=16)
                src_ = self.w_mod[l, :, n2 * 256:(n2 + 1) * 256].rearrange("(k p) n -> p k n", p=128)
                wk = ('ws', big)
                self.dma('pool', self.wl[big], wv, src_, (), [wk])
                for j in range(2):
                    n = n2 * 2 + j
                    b = self.bank()
                    pk = ('ps', b)
                    for k in range(16):
                        self.mm(self.ps[b][:, 0:NSEQ], wv[:, k, j * 128:(j + 1) * 128], scT3[:, k, :],
                                k == 0, k == 15, [wk, 'scT'], [pk])
                    o = n * NSEQ
                    self.act(self.modT[:, o:o + NSEQ], self.ps[b][:, 0:NSEQ], AF.Identity, [pk, 'bmT'],
                             ['modT'], bias=self.bmT[:, l * 96 + n:l * 96 + n + 1])
            self.dma('sp', self.misc[3], self.MODS[l], self.modT[:, :], ['modT'], [('MODS', l)])
        self.s.fence()

    def layer_vecs(self, l):
        sq = float(np.sqrt(D))
        self.dma('sp', self.misc[3], self.modT[:, :], self.MODS[l], [('MODS', l)], ['modT'])
        m = self.modT[:, :].rearrange("p (j c q) -> p j c q", j=6, q=NSEQ)
        v = self.vecs2[l % 2][:, :].rearrange("p (j c q) -> p j c q", j=6, q=NSEQ)
        g = self.gT[:, l * 64:(l + 1) * 64].rearrange("p (j c) -> p j c", j=4)
        rk = ['modT', 'gT']
        wk = [('vecs',)]
        self.cur_vecs = self.vecs2[l % 2]

        def gb(j):
            return g[:, j, :].unsqueeze(2).broadcast_to([128, 16, NSEQ])
        self.stt('dve', v[:, 0], m[:, 1], 1.0, gb(0), ALU.add, ALU.mult, rk, wk)
        self.ts('dve', v[:, 0], v[:, 0], sq, None, ALU.mult, None, wk, wk)
        self.copy('dve', v[:, 1], m[:, 0], rk, wk)
        self.stt('dve', v[:, 2], m[:, 2], sq, gb(1), ALU.mult, ALU.mult, rk, wk)
        self.stt('dve', v[:, 3], m[:, 4], 1.0, gb(2), ALU.add, ALU.mult, rk, wk)
        self.ts('dve', v[:, 3], v[:, 3], sq, None, ALU.mult, None, wk, wk)
        self.copy('dve', v[:, 4], m[:, 3], rk, wk)
        self.stt('dve', v[:, 5], m[:, 5], sq, gb(3), ALU.mult, ALU.mult, rk, wk)
        self.v4 = v

    def stats_rstd(self, src3, W, rk, tag, wsq, wrs):
        sqv, sqk = wsq
        rs, rsk = wrs
        self.act(sqv, src3, AF.Square, rk, [sqk])
        b = self.bank()
        pk = ('ps', b)
        for c in range(16):
            self.mm(self.ps[b][:, 0:W], self.ones_bf[:, :], sqv[:, c, :], c == 0, c == 15, [sqk, 'const'], [pk])
        self.act(rs, self.ps[b][:, 0:W], AF.Sqrt, [pk], [rsk], bias=self.epsD[:, 0:1])
        self.s.add('dve', lambda e: e.reciprocal(rs, rs), [rsk], [rsk])

    def seq_cols(self, col0, W):
        if col0 < TP:
            return [(0, 0, W)]
        out = []
        for sidx in range(4):
            out.append((1 + sidx, sidx * 32, 32))
        return out

    def make_h(self, xt3, xk, col0, W, ja, jb, ws):
        sqv, sqk, rs, rsk, tmp, tmpk = ws
        self.stats_rstd(xt3, W, [xk], 'h', (sqv, sqk), (rs, rsk))
        self.tt('dve', tmp, xt3, rs.unsqueeze(1).broadcast_to([128, 16, W]), ALU.mult, [xk, rsk], [tmpk])
        v = self.v4
        for c in range(16):
            for (q, lo, w) in self.seq_cols(col0, W):
                eng = 'act' if c % 2 == 0 else 'pool'
                hk = ('H', c, col0)
                out = self.H3[:, c, col0 + lo:col0 + lo + w]
                if eng == 'act':
                    self.act(out, tmp[:, c, lo:lo + w], AF.Identity, [tmpk, ('vecs',)], [hk],
                             bias=v[:, jb, c, q:q + 1], scale=v[:, ja, c, q:q + 1])
                else:
                    self.ts('pool', out, tmp[:, c, lo:lo + w], v[:, ja, c, q:q + 1], v[:, jb, c, q:q + 1],
                            ALU.mult, ALU.add, [tmpk, ('vecs',)], [hk])

    def upd_ws(self, W, par):
        Rf = self.R.bitcast(F32)
        base = par * 11264
        x3 = Rf[:, base:base + 16 * W].rearrange("p (c w) -> p c w", c=16)
        y3 = Rf[:, base + 4096:base + 4096 + 16 * W].rearrange("p (c w) -> p c w", c=16)
        rs = Rf[:, base + 8192:base + 8192 + W]
        rs2 = Rf[:, base + 8448:base + 8448 + W]
        sqb = self.R[:, 2 * base + 17920:2 * base + 17920 + 16 * W].rearrange("p (c w) -> p c w", c=16)
        return x3, y3, rs, rs2, sqb

    def prologue(self):
        self.layer_vecs(0)
        Rf = self.R.bitcast(F32)
        n = 0
        for j in range(17):
            par = n % 2
            n += 1
            col0 = j * 128
            x3, y3, rs, rs2, sqb = self.upd_ws(128, par)
            xtm = Rf[:, par * 11264 + 4096:par * 11264 + 4096 + 2048]
            tk = ('xtm', par)
            src = self.xp[col0:col0 + 128, :] if j < 16 else self.xs[:, :]
            self.dma('sp', self.sp_ld[par], xtm, src, (), [tk])
            xk = ('xt', par)
            for g4 in range(4):
                b = self.bank()
                pk = ('ps', b)
                for cc in range(4):
                    c = g4 * 4 + cc
                    self.tr(self.ps[b][:, cc * 128:(cc + 1) * 128], xtm[:, c * 128:(c + 1) * 128],
                            self.ident_f[:, :], [tk, 'id'], [pk])
                eng = 'dve' if g4 % 2 == 0 else 'act'
                self.copy(eng, x3[:, g4 * 4:(g4 + 1) * 4, :],
                          self.ps[b][:, :].rearrange("p (c w) -> p c w", c=4), [pk], [xk])
            self.dma('sp', self.sp_st[par], self.X.rearrange("c p t -> p c t")[:, :, col0:col0 + 128], x3,
                     [xk], [('X', col0)])
            tmp = y3
            self.make_h(x3, xk, col0, 128, 0, 1, (sqb, ('sq', par), rs, ('rs', par), tmp, tk))
        self.s.fence()

    def linear(self, wsrc, nchunks, kc, rhs_fn, tiles, epi):
        for j in range(nchunks):
            sub = self.wcnt % 4
            self.wcnt += 1
            wv = self.ws_view(sub, kc)
            wk = ('ws', sub)
            self.dma('pool', self.wl[sub], wv, wsrc(j).rearrange("(k p) n -> p k n", p=128), (), [wk])
            for (col0, W) in tiles:
                b = self.bank()
                pk = ('ps', b)
                for k in range(kc):
                    self.mm(self.ps[b][:, 0:W], wv[:, k, :], rhs_fn(k, col0, W), k == 0, k == kc - 1, [wk], [pk])
                epi(j, col0, W, b, pk)

    def y_store_epi(self, accumulate=False):
        st = {'n': 0}
        Tm = self.TMP

        def epi(j, col0, W, b, pk):
            par = st['n'] % 2
            st['n'] += 1
            stg = Tm[:, par * 512:par * 512 + W]
            sk = ('ystg', par)
            ydst = self.Y[j, :, col0:col0 + W]
            if accumulate:
                prev = Tm[:, 1024 + par * 512:1024 + par * 512 + W]
                pvk = ('yprev', par)
                self.dma('sp', self.sp_ld[par], prev, ydst, [('Y', j, col0)], [pvk])
                self.tt('dve', stg, self.ps[b][:, 0:W], prev, ALU.add, [pk, pvk], [sk])
            else:
                self.copy('act' if par == 0 else 'dve', stg, self.ps[b][:, 0:W], [pk], [sk])
            self.dma('sp', self.sp_st[par], ydst, stg, [sk], [('Y', j, col0)])
        return epi

    def update_pass(self, l, which):
        final = (l == self.nl - 1 and which == 1)
        v = self.v4
        jg = 2 if which == 0 else 5
        if which == 1 and not final:
            self.layer_vecs(l + 1)
            ja, jb, vn = 0, 1, self.v4
        else:
            ja, jb, vn = 3, 4, v
        tiles = [(c0, 256) for c0 in range(0, TP, 256)] + [(TP, 128)]
        for ti, (col0, W) in enumerate(tiles):
            par = ti % 2
            x3, y3, rs, rs2, sqb = self.upd_ws(W, par)
            xk, yk, sqk, rsk, rs2k = ('ux', par), ('uy', par), ('usq', par), ('urs', par), ('urs2', par)
            self.dma('sp', self.sp_ld[2 + par], x3, self.X.rearrange("c p t -> p c t")[:, :, col0:col0 + W], (), [xk])
            self.dma('sp', self.sp_ld[4 + par], y3, self.Y.rearrange("c p t -> p c t")[:, :, col0:col0 + W], (), [yk])
            self.stats_rstd(y3, W, [yk], 'u', (sqb, sqk), (rs, rsk))
            self.tt('dve', y3, y3, rs.unsqueeze(1).broadcast_to([128, 16, W]), ALU.mult, [yk, rsk], [yk])
            for c in range(16):
                for (q, lo, w) in self.seq_cols(col0, W):
                    eng = 'dve'
                    self.stt(eng, x3[:, c, lo:lo + w], y3[:, c, lo:lo + w], v[:, jg, c, q:q + 1], x3[:, c, lo:lo + w],
                             ALU.mult, ALU.add, [yk, xk, ('vecs',)], [xk])
            if not final:
                self.dma('sp', self.sp_st[2 + par], self.X.rearrange("c p t -> p c t")[:, :, col0:col0 + W], x3,
                         [xk], [('X', col0)])
                self.v4 = vn
                self.make_h(x3, xk, col0, W, ja, jb, (sqb, sqk, rs2, rs2k, y3, yk))
                self.v4 = v
            else:
                for sbk in range(W // 128):
                    ytm = y3.rearrange("p c w -> p (c w)")[:, sbk * 2048:(sbk + 1) * 2048] if W == 256 else \
                        y3.rearrange("p c w -> p (c w)")
                    for g4 in range(4):
                        b = self.bank()
                        pk = ('ps', b)
                        for cc in range(4):
                            c = g4 * 4 + cc
                            self.tr(self.ps[b][:, cc * 128:(cc + 1) * 128], x3[:, c, sbk * 128:(sbk + 1) * 128],
                                    self.ident_f[:, :], [xk, 'id'], [pk])
                        self.copy('dve' if g4 % 2 == 0 else 'act', ytm[:, g4 * 512:(g4 + 1) * 512], self.ps[b][:, :],
                                  [pk, yk], [yk])
                    if col0 < TP:
                        dst = self.yp[col0 + sbk * 128:col0 + (sbk + 1) * 128, :]
                    else:
                        dst = self.ys[:, :]
                    self.dma('sp', self.sp_st[2 + par], dst, ytm, [yk], [('yo', col0, sbk)])
        self.v4 = vn
        self.s.fence()

    def conv3(self, eng, Cb, Pb_views, wv, rk, wk):
        self.ts(eng, Cb, Pb_views[0], wv[0], None, ALU.mult, None, rk, wk)
        self.stt(eng, Cb, Pb_views[1], wv[1], Cb, ALU.mult, ALU.add, rk + wk, wk)
        self.stt(eng, Cb, Pb_views[2], wv[2], Cb, ALU.mult, ALU.add, rk + wk, wk)

    def mixer_even(self, l):
        i = l // 2
        R = self.R
        Rf = R.bitcast(F32)
        O3 = R[:, 0:16 * T].rearrange("p (c t) -> p c t", c=16)
        E0 = 16 * T // 2
        Tm = self.TMP
        s = self.s
        wvr = R[:, 8 * T:8 * T + 16384].rearrange("p (k n) -> p k n", k=16)
        for jj in range(4):
            self.dma('pool', self.wl[jj], wvr[:, :, jj * 256:(jj + 1) * 256],
                     self.w_in[i, :, 4096 + jj * 256:4096 + (jj + 1) * 256].rearrange("(k p) n -> p k n", p=128),
                     (), [('wvr', jj)])
        gbc = Rf[:, E0:E0 + 1024]
        svf = Rf[:, E0 + 1024:E0 + 2048]
        junk = R[:, 2 * (E0 + 2048):2 * (E0 + 2048) + 512]
        self.dma('sp', self.misc[0], gbc, self.g_sgu[i, :].partition_broadcast(128), (), ['gbc'])
        self.ts('dve', gbc, gbc, 32.0, None, ALU.mult, None, ['gbc'], ['gbc'])
        ssv = Tm[:, 0:4]
        for j in range(17):
            pks = []
            self.memset('pool', ssv[:, 0:2], 0.0, ['ssv'])
            bs_ = []
            for nh in range(2):
                b = self.bank()
                pk = ('ps', b)
                bs_.append((b, pk))
                for k in range(16):
                    self.mm(self.ps[b][:, :], self.H3[:, k, j * 128:(j + 1) * 128], wvr[:, k, nh * 512:(nh + 1) * 512],
                            k == 0, k == 15, [('wvr', 2 * nh), ('wvr', 2 * nh + 1)], [pk])
                self.act(junk, self.ps[b][:, :], AF.Square, [pk, 'ssv'], ['junk', 'ssv'], accum_out=ssv[:, nh:nh + 1])
            self.tt('dve', ssv[:, 2:3], ssv[:, 0:1], ssv[:, 1:2], ALU.add, ['ssv'], ['ssv2'])
            self.act(ssv[:, 3:4], ssv[:, 2:3], AF.Sqrt, ['ssv2'], ['ssv3'], bias=self.epsD[:, 1:2])
            s.add('dve', lambda e: e.reciprocal(ssv[:, 3:4], ssv[:, 3:4]), ['ssv3'], ['ssv3'])
            for nh in range(2):
                b, pk = bs_[nh]
                self.stt('dve', R[:, j * 1024 + nh * 512:j * 1024 + (nh + 1) * 512], self.ps[b][:, :], ssv[:, 3:4],
                         gbc[:, nh * 512:(nh + 1) * 512], ALU.mult, ALU.mult, [pk, 'ssv3', 'gbc'], [('vn', j)])
                if j == 16:
                    self.stt('dve', svf[:, nh * 512:(nh + 1) * 512], self.ps[b][:, :], ssv[:, 3:4],
                             gbc[:, nh * 512:(nh + 1) * 512], ALU.mult, ALU.mult, [pk, 'ssv3', 'gbc'], ['svf'])
        self.dma('sp', self.misc[1], self.sgv[i], svf, ['svf'], ['sgv'])
        s.fence()
        wstg = Rf[:, E0:E0 + 1024].rearrange("p (g j) -> p g j", g=8)
        sstg = Rf[:, E0 + 1024:E0 + 2048].rearrange("p (g j) -> p g j", g=8)
        wsT = R[:, 2 * (E0 + 2048):2 * (E0 + 2048) + 1024].rearrange("p (g i) -> p g i", g=8)
        wsS = R[:, 2 * (E0 + 2560):2 * (E0 + 2560) + 1024].rearrange("p (g i) -> p g i", g=8)
        brow = Rf[0:1, E0 + 3072:E0 + 4096]
        bhi = R[0:1, 2 * (E0 + 4096):2 * (E0 + 4096) + 1024]
        blo = R[0:1, 2 * (E0 + 4608):2 * (E0 + 4608) + 1024]
        bShi = R[0:1, 2 * (E0 + 5120):2 * (E0 + 5120) + 1024]
        bSlo = R[0:1, 2 * (E0 + 5632):2 * (E0 + 5632) + 1024]
        mxs = [Tm[:, 0:512], Tm[:, 512:1024]]
        self.dma('sp', self.misc[0], wstg, self.w_sgu[i].rearrange("g i j -> i g j"), (), ['wstg'])
        self.memset('pool', sstg, 0.0, ['sstg'])
        for sq_ in range(4):
            self.dma('sp', self.misc[1 + sq_ % 2], sstg[32 * sq_:32 * sq_ + 32, :, 32 * sq_:32 * sq_ + 32],
                     self.w_sgu[i, :, 0:32, 0:32].rearrange("g i j -> i g j"), ['sstg'], ['sstg'])
        for (stg, dst, kk) in ((wstg, wsT, 'wstg'), (sstg, wsS, 'sstg')):
            for g2 in range(2):
                b = self.bank()
                pk = ('ps', b)
                for gg in range(4):
                    g = g2 * 4 + gg
                    self.tr(self.ps[b][:, gg * 128:(gg + 1) * 128], stg[:, g, :], self.ident_f[:, :], [kk, 'id'], [pk])
                self.copy('dve', dst[:, g2 * 4:(g2 + 1) * 4, :], self.ps[b][:, :].rearrange("p (g i) -> p g i", g=4),
                          [pk], ['wsT'])
        self.memset('pool', wsT[64:128, :, 0:64], 0.0, ['wsT'])
        self.dma('sp', self.misc[3], brow, self.b_sgu[i].rearrange("g i -> (g i)").unsqueeze(0), (), ['brow'])
        self.copy('dve', bhi, brow, ['brow'], ['bhi'])
        self.tt('dve', blo, brow, bhi, ALU.subtract, ['brow', 'bhi'], ['blo'])
        for (src_, dst_) in ((bhi, bShi), (blo, bSlo)):
            self.copy('dve', dst_.rearrange("p (g s t) -> p g s t", g=8, s=4),
                      src_.rearrange("p (g i) -> p g i", g=8)[:, :, 0:32].unsqueeze(2).broadcast_to([1, 8, 4, 32]),
                      ['bhi', 'blo'], ['bS'])
        st = {'n': 0}
        ones_row = self.ones_bf[0:1, :]

        def epi_u(g, col0, W, b, pk):
            par = st['n'] % 2
            st['n'] += 1
            bm = self.bank()
            pm = ('ps', bm)
            for sbk in range(W // 128):
                j = col0 // 128 + sbk
                samp = (j == 16)
                ws_ = wsS if samp else wsT
                bh_, bl_ = (bShi, bSlo) if samp else (bhi, blo)
                o = self.ps[bm][:, sbk * 128:(sbk + 1) * 128]
                self.mm(o, R[:, j * 1024 + g * 128:j * 1024 + (g + 1) * 128], ws_[:, g, :], True, False,
                        [('vn', j), 'wsT'], [pm])
                self.mm(o, ones_row, bh_[:, g * 128:(g + 1) * 128], False, False, ['bhi', 'bS', 'const'], [pm])
                self.mm(o, ones_row, bl_[:, g * 128:(g + 1) * 128], False, True, ['blo', 'bS'], [pm])
            mk = ('mxs', par)
            self.copy('act', mxs[par][:, 0:W], self.ps[bm][:, 0:W], [pm], [mk])
            self.tt('dve', O3[:, 8 + g, col0:col0 + W], self.ps[b][:, 0:W], mxs[par][:, 0:W], ALU.mult, [pk, mk],
                    [('O', 8 + g, col0)])
        self.linear(lambda g: self.w_in[i, :, 3072 + g * 128:3072 + (g + 1) * 128], 8, 16,
                    lambda k, col0, W: self.H3[:, k, col0:col0 + W], TILES, epi_u)
        s.fence()
        wc = Tm[:, 3000:3024].rearrange("p (t c) -> p t c", t=3)
        for t in range(3):
            self.dma('sp', self.misc[t], wc[:, t, :], self.w_conv_a[i, t, :].rearrange("(c p) -> p c", p=128), (), ['wc'],
                     nc_ok=True)
        hist = Tm[:, 3024:3088].rearrange("p (c s t) -> p c s t", c=8, s=4)
        for c in range(8):
            for sq_ in range(4):
                self.dma('sp', self.misc[(c * 4 + sq_) % 4], hist[:, c, sq_, :],
                         self.sca[i, sq_, :, c * 128:(c + 1) * 128].rearrange("t p -> p t"), (), ['hist'], nc_ok=True)
        CS = Tm[:, 3088:3168].rearrange("p (c q t) -> p c q t", c=8, q=5)
        Pbs = [Rf[:, E0:E0 + 514], Rf[:, E0 + 514:E0 + 1028]]
        Cbs = [Rf[:, E0 + 1028:E0 + 1540], Rf[:, E0 + 1540:E0 + 2052]]
        xas = [Rf[:, E0 + 2052:E0 + 2564], Rf[:, E0 + 2564:E0 + 3076]]
        PS_ = Rf[:, E0 + 3076:E0 + 3076 + 136].rearrange("p (s t) -> p s t", s=4)
        st2 = {'n': 0, 'ps': {}}
        order = []
        for c in range(8):
            order += [c, 16 + c, 8 + c]

        def epi_a(jj, col0, W, b, pk):
            c = jj // 3
            kind = jj % 3
            st2['ps'][(kind, col0)] = (b, pk)
            return

        for c in range(8):
            wvs = []
            for kind, colidx in enumerate((c, 16 + c, 8 + c)):
                sub = self.wcnt % 4
                self.wcnt += 1
                wv = self.ws_view(sub, 16)
                wk = ('ws', sub)
                self.dma('pool', self.wl[sub], wv,
                         self.w_in[i, :, colidx * 128:(colidx + 1) * 128].rearrange("(k p) n -> p k n", p=128), (), [wk])
                wvs.append((wv, wk))
            for ti, (col0, W) in enumerate(TILES):
                par = st2['n'] % 2
                st2['n'] += 1
                bks = []
                for (wv, wk) in wvs:
                    b = self.bank()
                    pk = ('ps', b)
                    for k in range(16):
                        self.mm(self.ps[b][:, 0:W], wv[:, k, :], self.H3[:, k, col0:col0 + W], k == 0, k == 15, [wk], [pk])
                    bks.append((b, pk))
                (bx, pkx), (bc_, pkc), (bg, pkg) = bks
                xa = xas[par]
                xk_ = ('xa', par)
                Pb = Pbs[par]
                pbk = ('Pb', par)
                Cb = Cbs[par]
                cbk = ('Cb', par)
                self.copy('act', xa[:, 0:W], self.ps[bx][:, 0:W], [pkx], [xk_])
                wv3 = [wc[:, t, c:c + 1] for t in range(3)]
                if col0 < TP:
                    if ti == 0:
                        self.memset('pool', Pb[:, 0:2], 0.0, [pbk])
                    else:
                        self.copy('pool', Pb[:, 0:2], Pbs[1 - par][:, 512:514], [('Pb', 1 - par)], [pbk])
                    self.tt('dve', Pb[:, 2:2 + W], self.ps[bc_][:, 0:W], xa[:, 0:W], ALU.mult, [pkc, xk_], [pbk])
                    self.conv3('dve', Cb[:, 0:W], [Pb[:, t:t + W] for t in range(3)], wv3, [pbk, 'wc'], [cbk])
                    self.tt('dve', O3[:, c, col0:col0 + W], self.ps[bg][:, 0:W], Cb[:, 0:W], ALU.mult, [pkg, cbk],
                            [('O', c, col0)])
                    if col0 + W == TP:
                        self.copy('pool', CS[:, c, 0, :], Pb[:, W:W + 2], [pbk], ['CS'])
                else:
                    v4_ = lambda ap: ap.rearrange("p (s t) -> p s t", s=4)
                    self.copy('pool', PS_[:, :, 0:2], hist[:, c, :, :], ['hist'], ['PS_'])
                    self.tt('dve', PS_[:, :, 2:34], v4_(self.ps[bc_][:, 0:128]), v4_(xa[:, 0:128]), ALU.mult,
                            [pkc, xk_], ['PS_'])
                    Cs = v4_(Cb[:, 0:128])
                    self.conv3('dve', Cs, [PS_[:, :, t:t + 32] for t in range(3)], wv3, ['PS_', 'wc'], [cbk])
                    self.tt('dve', v4_(O3[:, c, TP:T]), v4_(self.ps[bg][:, 0:128]), Cs, ALU.mult, [pkg, cbk],
                            [('O', c, col0)])
                    self.copy('pool', CS[:, c, 1:5, :], PS_[:, :, 32:34], ['PS_'], ['CS'])
        for c in range(8):
            self.dma('sp', self.misc[c % 4], self.cap[i][:, c * 128:(c + 1) * 128].rearrange("t p -> p t"),
                     CS[:, c, 0, :], ['CS'], [('cap', c)], nc_ok=True)
            for sq_ in range(4):
                self.dma('sp', self.misc[(c + sq_) % 4],
                         self.cas[i, sq_][:, c * 128:(c + 1) * 128].rearrange("t p -> p t"),
                         CS[:, c, 1 + sq_, :], ['CS'], [('cas', c, sq_)], nc_ok=True)
        s.fence()
        self.linear(lambda n: self.w_out[i, :, n * 128:(n + 1) * 128], 16, 16,
                    lambda k, col0, W: O3[:, k, col0:col0 + W], TILES, self.y_store_epi())
        s.fence()

    def ffn(self, l):
        R = self.R
        Tm = self.TMP
        s = self.s
        A3 = R[:, 0:22 * T].rearrange("p (f t) -> p f t", f=22)
        wcv = Tm[:, 2400:2532].rearrange("p (t f) -> p t f", t=3)
        bcv = Tm[:, 2532:2576]
        FS = Tm[:, 2576:3016].rearrange("p (f q t) -> p f q t", f=44, q=5)
        hs = Tm[:, 3016:3368].rearrange("p (f s t) -> p f s t", f=44, s=4)
        for t in range(3):
            self.dma('sp', self.misc[t], wcv[:, t, :], self.w_fconv[l, t, :].rearrange("(f p) -> p f", p=128), (), ['wcv'],
                     nc_ok=True)
        self.dma('sp', self.misc[3], bcv, self.b_fconv[l, :].rearrange("(f p) -> p f", p=128), (), ['bcv'], nc_ok=True)
        for sq_ in range(4):
            for t in range(2):
                self.dma('sp', self.misc[(sq_ * 2 + t) % 4], hs[:, :, sq_, t],
                         self.sfc[l, sq_, t, :].rearrange("(f p) -> p f", p=128), (), ['hs'], nc_ok=True)
        Abs_ = [Tm[:, 0:514], Tm[:, 514:1028]]
        Cbs = [Tm[:, 1028:1540], Tm[:, 1540:2052]]
        AS_ = Tm[:, 2052:2188].rearrange("p (s t) -> p s t", s=4)
        for hf in range(2):
            st = {'n': 0, 'a': None}

            def epi_up(jj, col0, W, b, pk, hf=hf, st=st):
                ff = jj // 2
                f = hf * 22 + ff
                if jj % 2 == 0:
                    st['a'] = (b, pk)
                    return
                ba, pka = st['a']
                par = st['n'] % 2
                st['n'] += 1
                Ab = Abs_[par]
                abk = ('Ab', par)
                Cb = Cbs[par]
                cbk = ('Cb', par)
                wv3 = [wcv[:, t, f:f + 1] for t in range(3)]
                if col0 < TP:
                    if col0 == 0:
                        self.memset('pool', Ab[:, 0:2], 0.0, [abk])
                    else:
                        self.copy('pool', Ab[:, 0:2], Abs_[1 - par][:, 512:514], [('Ab', 1 - par)], [abk])
                    self.copy('act', Ab[:, 2:2 + W], self.ps[ba][:, 0:W], [pka], [abk])
                    self.conv3('dve', Cb[:, 0:W], [Ab[:, t:t + W] for t in range(3)], wv3, [abk, 'wcv'], [cbk])
                    self.act(Cb[:, 0:W], Cb[:, 0:W], AF.Silu, [cbk, 'bcv'], [cbk], bias=bcv[:, f:f + 1])
                    self.tt('dve', A3[:, ff, col0:col0 + W], Cb[:, 0:W], self.ps[b][:, 0:W], ALU.mult, [cbk, pk],
                            [('A', ff, col0)])
                    if col0 + W == TP:
                        self.copy('pool', FS[:, f, 0, :], Ab[:, W:W + 2], [abk], ['FS'])
                else:
                    v4_ = lambda ap: ap.rearrange("p (s t) -> p s t", s=4)
                    self.copy('pool', AS_[:, :, 0:2], hs[:, f, :, :], ['hs'], ['AS_'])
                    self.copy('act', AS_[:, :, 2:34], v4_(self.ps[ba][:, 0:128]), [pka], ['AS_'])
                    Cs = v4_(Cb[:, 0:128])
                    self.conv3('dve', Cs, [AS_[:, :, t:t + 32] for t in range(3)], wv3, ['AS_', 'wcv'], [cbk])
                    self.act(Cb[:, 0:128], Cb[:, 0:128], AF.Silu, [cbk, 'bcv'], [cbk], bias=bcv[:, f:f + 1])
                    self.tt('dve', A3[:, ff, TP:T], Cb[:, 0:128], self.ps[b][:, 0:128], ALU.mult, [cbk, pk],
                            [('A', ff, col0)])
                    self.copy('pool', FS[:, f, 1:5, :], AS_[:, :, 32:34], ['AS_'], ['FS'])

            def wsrc_up(jj, hf=hf):
                f = hf * 22 + jj // 2
                off = f * 128 + (DFF if jj % 2 == 1 else 0)
                return self.w_up[l, :, off:off + 128]
            for ff in range(22):
                wvs = []
                for jj in (2 * ff, 2 * ff + 1):
                    sub = self.wcnt % 4
                    self.wcnt += 1
                    wv = self.ws_view(sub, 16)
                    wk = ('ws', sub)
                    self.dma('pool', self.wl[sub], wv, wsrc_up(jj).rearrange("(k p) n -> p k n", p=128), (), [wk])
                    wvs.append((wv, wk))
                for (col0, W) in TILES:
                    for jj, (wv, wk) in zip((2 * ff, 2 * ff + 1), wvs):
                        b = self.bank()
                        pk = ('ps', b)
                        for k in range(16):
                            self.mm(self.ps[b][:, 0:W], wv[:, k, :], self.H3[:, k, col0:col0 + W], k == 0, k == 15,
                                    [wk], [pk])
                        epi_up(jj, col0, W, b, pk)
            s.fence()
            self.linear(lambda n, hf=hf: self.w_down[l, hf * 2816:(hf + 1) * 2816, n * 128:(n + 1) * 128], 16, 22,
                        lambda k, col0, W: A3[:, k, col0:col0 + W], TILES, self.y_store_epi(accumulate=(hf == 1)))
            s.fence()
        for t in range(2):
            self.dma('sp', self.misc[t], self.fcp[l, t, :].rearrange("(f p) -> p f", p=128), FS[:, :, 0, t], ['FS'],
                     [('fcp', t)], nc_ok=True)
            for sq_ in range(4):
                self.dma('sp', self.misc[(t + sq_) % 4], self.fcs[l, sq_, t, :].rearrange("(f p) -> p f", p=128),
                         FS[:, :, 1 + sq_, t], ['FS'], [('fcs', sq_, t)], nc_ok=True)
        s.fence()

    def mixer_odd(self, l):
        if self.odd_mode == 'skip':
            return
        i = l // 2
        R = self.R
        Tm = self.TMP
        Tb = Tm.bitcast(BF16)
        s = self.s
        O3 = R[:, 0:16 * T].rearrange("p (c t) -> p c t", c=16)
        st = {'n': 0}

        def epi_qkv(j, col0, W, b, pk):
            par = st['n'] % 2
            st['n'] += 1
            kind, h = j // 16, j % 16
            kf = Tm[:, 1024 + par * 512:1024 + par * 512 + W]
            kk = ('kf', par)
            if kind >= 1:
                self.copy('dve', kf, self.ps[b][:, 0:W], [pk], [kk])
            if kind < 2:
                stg = Tb[:, par * 512:par * 512 + W]
                sk = ('qstg', par)
                if kind == 0:
                    self.copy('act', stg, self.ps[b][:, 0:W], [pk], [sk])
                else:
                    self.copy('act', stg, kf, [kk], [sk])
                dst = (self.QS if kind == 0 else self.KS)[h, :, col0:col0 + W]
                self.dma('sp', self.sp_st[par], dst, stg, [sk], [('QK', kind, h, col0)])
            if kind >= 1:
                b2 = self.bank()
                pk2 = ('ps', b2)
                nsb = W // 128
                for sbk in range(nsb):
                    self.tr(self.ps[b2][:, sbk * 128:(sbk + 1) * 128], kf[:, sbk * 128:(sbk + 1) * 128],
                            self.ident_f[:, :], [kk, 'id'], [pk2])
                ktm = Tm[:, 2048 + par * 512:2048 + par * 512 + W]
                tk = ('ktm', par)
                self.copy('act' if par == 0 else 'dve', ktm, self.ps[b2][:, 0:W], [pk2], [tk])
                if col0 < TP:
                    dst = (self.kp if kind == 1 else self.vp)[i, col0:col0 + W, h * 128:(h + 1) * 128] \
                        .rearrange("(s p) d -> p s d", p=128)
                else:
                    dst = (self.ks if kind == 1 else self.vs)[i, :, h * 128:(h + 1) * 128].unsqueeze(1)
                self.dma('sp', self.sp_st[2 + par], dst, ktm.rearrange("p (s d) -> p s d", d=128), [tk],
                         [('KV', kind, h, col0)])
        self.linear(lambda j: self.w_qkv[i, :, j * 128:(j + 1) * 128], 48, 16,
                    lambda k, col0, W: self.H3[:, k, col0:col0 + W], TILES, epi_qkv)
        s.fence()
        if self.odd_mode == 'qkv':
            return
        Hh = self.H
        Hf = Hh.bitcast(F32)
        qTs = [Hh[:, p_ * 6400:p_ * 6400 + T] for p_ in range(2)]
        kTs = [Hh[:, p_ * 6400 + T:p_ * 6400 + 2 * T] for p_ in range(2)]
        Vts = [Hh[:, p_ * 6400 + 2 * T:p_ * 6400 + 2 * T + 2048].rearrange("p (b d) -> p b d", b=16) for p_ in range(2)]
        masks = Hh[:, 12800:14848].rearrange("p (r t) -> p r t", r=4)
        ones512 = Hh[:, 14848:15360]
        tb = 15360
        Rf_ = R.bitcast(F32)
        NPAR = 4

        def tmp_views(base_bf, buf_bf, buf_f):
            return (buf_f[:, base_bf // 2:base_bf // 2 + 512], buf_bf[:, base_bf + 1024:base_bf + 1536],
                    buf_bf[:, base_bf + 1536:base_bf + 2048],
                    buf_f[:, (base_bf + 2048) // 2:(base_bf + 2048) // 2 + 512], buf_bf[:, base_bf + 3072:base_bf + 3584])
        tv = [tmp_views(tb, Hh, Hf), tmp_views(tb + 3584, Hh, Hf),
              tmp_views(16 * T, R, Rf_), tmp_views(16 * T + 3584, R, Rf_)]
        es = [t_[0] for t_ in tv]
        sps = [t_[1] for t_ in tv]
        spms = [t_[2] for t_ in tv]
        tts = [t_[3] for t_ in tv]
        wss = [t_[4] for t_ in tv]
        Sb = Hh[:, 22528:23040]
        KCs = [Hh[:, 23040 + p_ * 1024:23040 + (p_ + 1) * 1024].rearrange("p (b d) -> p b d", b=8) for p_ in range(2)]
        KcT = [Hh[:, 25088 + q_ * 1024:25088 + (q_ + 1) * 1024] for q_ in range(4)]
        Vc = [Hh[:, 29184 + q_ * 1024:29184 + (q_ + 1) * 1024].rearrange("p (b d) -> p b d", b=8) for q_ in range(4)]
        Vsn = Hh[0:32, 33280:33792].rearrange("p (q d) -> p q d", q=4)
        self.memset('pool', ones512, 1.0, ['ones512'])
        self.memset('pool', self.epsD[:, 2:3], 1.0, ['one1'])
        for r in range(4):
            self.asel(masks[:, r, :], ones512, [[1, 512]], ALU.is_gt, -128 * r, -1, ['ones512'], ['masks'])
        sc = float(128 ** -0.5)
        one1 = self.epsD[:, 2:3]
        cnt = {'n': 0}

        Sbufs = [Sb, Hh[:, 33792:34304], Hh[:, 34304:34816], Tb[:, 0:512], Tb[:, 512:1024]]
        NS = len(Sbufs)
        LA = 3
        v3 = lambda ap: ap.rearrange("p (q t) -> p q t", q=4)

        def run_blocks(blks, ob, ok, use_start):
            n = len(blks)
            stt_ = [None] * n
            for sbuf in Sbufs:
                self.memset('pool', sbuf, 0.0, [('S', id(sbuf))])

            def A(k):
                bl = blks[k]
                P, c0, Wc = bl['P'], bl['c0'], bl['Wc']
                tp = cnt['n'] % NPAR
                cnt['n'] += 1
                zb = self.bank([2, 3, 4])
                zk = ('ps', zb)
                for (o_, l_, r_, rk_) in bl['zmm'](zb):
                    self.mm(o_, l_, r_, True, True, rk_, [zk])
                e, sp_, spm, tt_, w_ = es[tp], sps[tp], spms[tp], tts[tp], wss[tp]
                ek, spk, smk, tk_, wk_ = ('e', tp), ('sp', tp), ('spm', tp), ('tt', tp), ('w', tp)
                z = self.ps[zb][0:P, c0:c0 + Wc]
                self.act(e[0:P, c0:c0 + Wc], z, AF.Exp, [zk], [ek], scale=sc)
                self.act(sp_[0:P, c0:c0 + Wc], e[0:P, c0:c0 + Wc], AF.Ln, [ek, 'one1'], [spk], bias=one1[0:P, :])
                mk = bl['mask']
                vw = v3 if bl.get('m3') else (lambda ap: ap)
                if mk is not None:
                    self.tt('pool', vw(spm[0:P, c0:c0 + Wc]), vw(sp_[0:P, c0:c0 + Wc]), mk, ALU.mult,
                            [spk, 'masks'], [smk])
                    spm_v, smk_r = spm, smk
                else:
                    spm_v, smk_r = sp_, spk
                self.stt('dve', tt_[0:P, c0:c0 + Wc], z, sc, sp_[0:P, c0:c0 + Wc], ALU.mult, ALU.subtract,
                         [zk, spk], [tk_])
                if k + 1 < n:
                    Sn = Sbufs[(k + 1) % NS]
                    Sc = Sbufs[k % NS]
                    if k == 0:
                        self.copy('pool', Sn[0:P, c0:c0 + Wc], spm_v[0:P, c0:c0 + Wc], [smk_r], [('S', id(Sn))])
                    else:
                        self.tt('pool', Sn[0:P, c0:c0 + Wc], Sc[0:P, c0:c0 + Wc], spm_v[0:P, c0:c0 + Wc], ALU.add,
                                [smk_r, ('S', id(Sc))], [('S', id(Sn))])
                stt_[k] = (tp, spm_v, smk_r)

            def B(k):
                bl = blks[k]
                P, c0, Wc = bl['P'], bl['c0'], bl['Wc']
                tp, spm_v, smk_r = stt_[k]
                tt_, w_ = tts[tp], wss[tp]
                tk_, wk_ = ('tt', tp), ('w', tp)
                lb = self.bank([5, 6, 7])
                lk = ('ps', lb)
                lat = self.ps[lb][0:P, c0:c0 + Wc]
                self.mm(lat, self.ustrict[0:P, 0:P], spm_v[0:P, c0:c0 + Wc], True, k == 0, [smk_r, 'id'], [lk])
                if k > 0:
                    Sc = Sbufs[k % NS]
                    self.mm(lat, self.ones_bf[:, 0:P], Sc[:, c0:c0 + Wc], False, True, [('S', id(Sc)), 'const'], [lk])
                self.tt('dve', tt_[0:P, c0:c0 + Wc], tt_[0:P, c0:c0 + Wc], lat, ALU.subtract, [tk_, lk], [tk_])
                self.act(w_[0:P, c0:c0 + Wc], tt_[0:P, c0:c0 + Wc], AF.Exp, [tk_], [wk_])
                mk = bl['mask']
                vw = v3 if bl.get('m3') else (lambda ap: ap)
                if mk is not None:
                    self.tt('pool', vw(w_[0:P, c0:c0 + Wc]), vw(w_[0:P, c0:c0 + Wc]), mk, ALU.mult, [wk_, 'masks'], [wk_])
                for (lhsT, lrk, oc0, wc0, ww) in bl['avs']:
                    self.mm(self.ps[ob][:, oc0:oc0 + ww], lhsT, w_[0:P, wc0:wc0 + ww], use_start and k == 0,
                            k == n - 1, [wk_] + lrk, [ok])
            for k in range(min(LA, n)):
                A(k)
            for k in range(n):
                if k + LA < n:
                    A(k + LA)
                B(k)

        for h in range(16):
            hp = h % 2
            qT, kT, Vt = qTs[hp], kTs[hp], Vts[hp]
            qk, kk_, vk = ('qT', hp), ('kT', hp), ('Vt', hp)
            self.dma('sp', self.sp_ld[hp], qT, self.QS[h], (), [qk])
            self.dma('sp', self.sp_ld[2 + hp], kT, self.KS[h], (), [kk_])
            self.dma('pool', self.wl[hp], Vt,
                     self.vp[i, :, h * 128:(h + 1) * 128].rearrange("(b p) d -> p b d", p=128), (), [vk])
            for qt in (range(4) if self.odd_mode != 'noprompt' else []):
                ob = self.bank([0, 1])
                ok = ('ps', ob)
                blks = []
                kmax = 4 * qt + 3
                for kb in range(kmax, -1, -1):
                    r = kb - 4 * qt
                    c0 = 128 * r if r > 0 else 0
                    Wc = 512 - c0

                    def zmm(zb, kb=kb, c0=c0, qt=qt):
                        return [(self.ps[zb][:, c0:512], kT[:, kb * 128:(kb + 1) * 128],
                                 qT[:, qt * 512 + c0:(qt + 1) * 512], [qk, kk_])]
                    blks.append(dict(zmm=zmm, P=128, c0=c0, Wc=Wc,
                                     mask=(masks[:, r, c0:512] if r >= 0 else None),
                                     avs=[(Vt[:, kb, :], [vk], c0, c0, Wc)]))
                run_blocks(blks, ob, ok, True)
                self.copy('act', O3[:, h, qt * 512:(qt + 1) * 512], self.ps[ob][:, :], [ok], [('O', h, qt)])
            if self.odd_mode == 'nosample':
                continue
            for q_ in range(4):
                kc_ = KCs[q_ % 2]
                kck = ('KC', q_ % 2)
                self.dma('pool', self.wl[2 + q_ % 2], kc_,
                         self.ck[i, q_, :, h * 128:(h + 1) * 128].rearrange("(b p) d -> p b d", p=128), (), [kck])
                tb_ = self.bank([2, 3, 4])
                tkk = ('ps', tb_)
                psb = self.ps[tb_].bitcast(BF16)
                for bb in range(8):
                    self.tr(psb[:, bb * 128:(bb + 1) * 128], kc_[:, bb, :], self.ident_b[:, :], [kck, 'id'], [tkk])
                self.copy('dve' if q_ % 2 == 0 else 'act', KcT[q_], psb[:, :], [tkk], [('KcT', q_)])
                self.dma('pool', self.pl[q_], Vc[q_],
                         self.cv[i, q_, :, h * 128:(h + 1) * 128].rearrange("(b p) d -> p b d", p=128), (), [('Vc', q_)])
                self.dma('pool', self.pl[q_], Vsn[:, q_, :], self.vs[i, 32 * q_:32 * q_ + 32, h * 128:(h + 1) * 128],
                         (), [('Vsn',)])
            ob = self.bank([0, 1])
            ok = ('ps', ob)
            self.memset('dve', self.ps[ob][:, 0:128], 0.0, [ok])
            blks = []

            def zmm_new(zb):
                return [(self.ps[zb][0:32, q_ * 32:(q_ + 1) * 32], kT[:, TP + 32 * q_:TP + 32 * q_ + 32],
                         qT[:, TP + 32 * q_:TP + 32 * q_ + 32], [qk, kk_]) for q_ in range(4)]
            mS = masks[0:32, 0, 0:32].unsqueeze(1).broadcast_to([32, 4, 32])
            blks.append(dict(zmm=zmm_new, P=32, c0=0, Wc=128, mask=mS, m3=True,
                             avs=[(Vsn[:, q_, :], [('Vsn',)], q_ * 32, q_ * 32, 32) for q_ in range(4)]))
            for kb in range(7, -1, -1):
                def zmm_c(zb, kb=kb):
                    return [(self.ps[zb][:, q_ * 32:(q_ + 1) * 32], KcT[q_][:, kb * 128:(kb + 1) * 128],
                             qT[:, TP + 32 * q_:TP + 32 * q_ + 32], [qk, ('KcT', q_)]) for q_ in range(4)]
                blks.append(dict(zmm=zmm_c, P=128, c0=0, Wc=128, mask=None,
                                 avs=[(Vc[q_][:, kb, :], [('Vc', q_)], q_ * 32, q_ * 32, 32) for q_ in range(4)]))
            run_blocks(blks, ob, ok, False)
            self.copy('act', O3[:, h, TP:T], self.ps[ob][:, 0:128], [ok], [('O', h, 4)])
        s.fence()
        self.linear(lambda n: self.w_o[i, :, n * 128:(n + 1) * 128], 16, 16,
                    lambda k, col0, W: O3[:, k, col0:col0 + W], TILES, self.y_store_epi())
        s.fence()


def make_in_maps(inp):
    f = lambda a: np.ascontiguousarray(np.asarray(a, dtype=np.float32))
    shared = {}
    for k in ['w_mod', 'b_mod', 'norm_g', 'w_in_ab', 'w_conv_a', 'g_sgu', 'w_sgu', 'b_sgu', 'w_out_ab',
              'w_qkv_sb', 'w_o_sb', 'w_ffn_up', 'w_ffn_conv', 'b_ffn_conv', 'w_ffn_down']:
        shared[k] = f(inp[k])
    maps = []
    for b in range(8):
        m = dict(shared)
        sl = slice(4 * b, 4 * b + 4)
        m['xp'] = f(inp['x_prompt'][b])
        m['xs'] = f(inp['x_sample'][sl]).reshape(TS, D)
        m['cc'] = f(np.concatenate([np.asarray(inp['c_prompt'])[b:b + 1], np.asarray(inp['c_sample'])[sl]], axis=0))
        m['ck'] = f(np.asarray(inp['cache_sb_k'])[:, sl]).reshape(2, 4, 1024, D)
        m['cv'] = f(np.asarray(inp['cache_sb_v'])[:, sl]).reshape(2, 4, 1024, D)
        m['sca'] = f(np.asarray(inp['state_conv_a'])[:, sl])
        m['sfc'] = f(np.asarray(inp['state_ffn_conv'])[:, sl])
        maps.append(m)
    return maps


_NC_CACHE = {}


def kernel(**inputs):
    if 'nc' not in _NC_CACHE:
        _NC_CACHE['nc'] = Builder().build()
    nc = _NC_CACHE['nc']
    maps = make_in_maps(inputs)
    res = run_bass_kernel_spmd(nc, maps, core_ids=list(range(8)))
    r = res.results
    cat = lambda k, ax: np.stack([np.asarray(x[k]) for x in r], axis=ax)
    y_prompt = cat('yp', 0)
    y_sample = cat('ys', 0).reshape(32, 32, D)
    k_p = cat('kp', 1).reshape(2, 8, TP, 16, 128)
    v_p = cat('vp', 1).reshape(2, 8, TP, 16, 128)
    ca_p = cat('cap', 1)
    f_p = cat('fcp', 1)
    k_s = cat('ks', 1).reshape(2, 32, 32, 16, 128)
    v_s = cat('vs', 1).reshape(2, 32, 32, 16, 128)
    ca_s = cat('cas', 1).reshape(2, 32, 2, 1024)
    f_s = cat('fcs', 1).reshape(4, 32, 2, DFF)
    sg = cat('sgv', 1).reshape(2, 32, 32, 1024)
    return (y_prompt, y_sample, k_p, v_p, ca_p, f_p, k_s, v_s, ca_s, f_s, sg)
```

```python
import numpy as np
import concourse.bass as bass
import concourse.mybir as mybir
from concourse.bass_utils import run_bass_kernel_spmd

F32 = mybir.dt.float32
BF16 = mybir.dt.bfloat16
ALU = mybir.AluOpType
AF = mybir.ActivationFunctionType

D = 2048
TP = 2048
TS = 128
T = TP + TS
NSEQ = 5
DFF = 5632
NFC = DFF // 128
DEPTH = 4
EPS = 1e-6
TILES = [(0, 512), (512, 512), (1024, 512), (1536, 512), (2048, 128)]
ENG = ['pe', 'act', 'dve', 'pool', 'sp']


class DmaSlot:
    def __init__(self, sem):
        self.sem = sem
        self.count = 0
        self.last = None


class Op:
    __slots__ = ('eng', 'idx', 'fn', 'dma', 'dma_val', 'waits', 'signal', 'clk', 'sig_no')

    def __init__(self, eng, idx, fn, dma):
        self.eng = eng
        self.idx = idx
        self.fn = fn
        self.dma = dma
        self.dma_val = 0
        self.waits = []
        self.signal = False
        self.clk = None
        self.sig_no = 0


class Sched:
    def __init__(self):
        self.ops = {e: [] for e in ENG}
        self.lastw = {}
        self.readers = {}
        self.clock = {e: [-1] * 5 for e in ENG}
        self.dmaw = {e: {} for e in ENG}
        self.pending = {e: [] for e in ENG}

    def add(self, eng, fn, reads=(), writes=(), dma=None):
        deps = []
        for k in reads:
            w = self.lastw.get(k)
            if w is not None:
                deps.append((w, True))
        for k in writes:
            w = self.lastw.get(k)
            if w is not None:
                deps.append((w, False))
            for r in self.readers.get(k, ()):
                deps.append((r, False))
        for p in self.pending[eng]:
            deps.append((p, True))
        self.pending[eng] = []
        op = Op(eng, len(self.ops[eng]), fn, dma)
        if dma is not None:
            if dma.last is not None:
                deps.append((dma.last, True))
            dma.count += 16
            op.dma_val = dma.count
            dma.last = op
        ei = ENG.index(eng)
        clk = self.clock[eng]
        for d, raw in deps:
            if d is op:
                continue
            if d.dma is not None:
                cur = self.dmaw[eng].get(d.dma, 0)
                if d.dma_val > cur:
                    self.dmaw[eng][d.dma] = d.dma_val
                    op.waits.append(d)
            else:
                di = ENG.index(d.eng)
                if d.idx > clk[di]:
                    if d.eng == eng and (eng == 'pe' or (not raw and eng != 'pool')):
                        continue
                    op.waits.append(d)
                    d.signal = True
                    dc = d.clk
                    for j in range(5):
                        if dc[j] > clk[j]:
                            clk[j] = dc[j]
        c = list(clk)
        c[ei] = op.idx
        op.clk = c
        self.ops[eng].append(op)
        for k in reads:
            self.readers.setdefault(k, []).append(op)
        for k in writes:
            self.lastw[k] = op
            self.readers[k] = []
        return op

    def fence(self):
        lasts = []
        for e in ENG:
            if self.ops[e]:
                if e == 'sp':
                    continue
                lasts.append(self.ops[e][-1])
        best = {}
        for k, w in self.lastw.items():
            if w.dma is not None:
                c = best.get(id(w.dma))
                if c is None or w.dma_val > c.dma_val:
                    best[id(w.dma)] = w
        for k, rs in self.readers.items():
            for r in rs:
                if r.dma is not None:
                    c = best.get(id(r.dma))
                    if c is None or r.dma_val > c.dma_val:
                        best[id(r.dma)] = r
        lasts.extend(best.values())
        for e in ENG:
            self.pending[e] = list(lasts)
        self.lastw = {}
        self.readers = {}

    def emit(self, nc, block, sems, final_waits):
        for e in ENG:
            n = 0
            for op in self.ops[e]:
                if op.signal and op.dma is None:
                    n += 1
                    op.sig_no = n

        def body_for(e):
            def body(eng):
                for op in self.ops[e]:
                    for d in op.waits:
                        if d.dma is not None:
                            eng.wait_ge(d.dma.sem, d.dma_val)
                        else:
                            eng.wait_ge(sems[d.eng], d.sig_no)
                    ins = op.fn(eng)
                    if op.dma is not None:
                        ins.then_inc(op.dma.sem, 16)
                    elif op.signal:
                        ins.then_inc(sems[e], 1)
                if e == 'sp':
                    for slot in final_waits:
                        if slot.count > 0:
                            eng.wait_ge(slot.sem, slot.count)
            return body
        block.tensor(body_for('pe'))
        block.scalar(body_for('act'))
        block.vector(body_for('dve'))
        block.gpsimd(body_for('pool'))
        block.sync(body_for('sp'))


class Builder:
    def __init__(self, nlayers=DEPTH, debug=False):
        self.nl = nlayers
        self.debug = debug
        self.nc = bass.Bass("TRN2", target_bir_lowering=False)
        self.s = Sched()
        self.slots = []
        self.psn = 0
        self.wcnt = 0
        self.pooln = {}
        import os as _os
        self.odd_mode = _os.environ.get('ODD_MODE', 'full')
        self.uid = 0

    def din(self, name, shape):
        return self.nc.dram_tensor(name, list(shape), F32, kind="ExternalInput").ap()

    def dout(self, name, shape):
        return self.nc.dram_tensor(name, list(shape), F32, kind="ExternalOutput").ap()

    def slot(self, name):
        sem = self.stack.enter_context(self.nc.semaphore(name))
        sl = DmaSlot(sem)
        self.slots.append(sl)
        return sl

    def sb(self, name, shape, dt):
        return self.stack.enter_context(self.nc.sbuf_tensor(name, list(shape), dt))

    def key(self, p):
        self.uid += 1
        return (p, self.uid)

    def bank(self, pool=None):
        if pool is None:
            b = self.psn % 8
            self.psn += 1
            return b
        n = self.pooln.get(tuple(pool), 0)
        self.pooln[tuple(pool)] = n + 1
        return pool[n % len(pool)]

    def dma(self, q, slot, out, in_, reads=(), writes=(), nc_ok=False):
        if nc_ok:
            fn = lambda e: e.dma_start(out=out, in_=in_, allow_slow_non_contiguous=True)
        else:
            fn = lambda e: e.dma_start(out=out, in_=in_)
        return self.s.add(q, fn, reads, writes, dma=slot)

    def mm(self, out, lhsT, rhs, start, stop, reads, writes):
        return self.s.add('pe', lambda e: e.matmul(out, lhsT=lhsT, rhs=rhs, start=start, stop=stop, skip_group_check=True),
                          reads, writes)

    def tr(self, out, in_, ident, reads, writes):
        return self.s.add('pe', lambda e: e.transpose(out, in_, ident), reads, writes)

    def act(self, out, in_, func, reads, writes, bias=None, scale=None, accum_out=None):
        kw = {}
        if bias is not None:
            kw['bias'] = bias
        if scale is not None:
            kw['scale'] = scale
        if accum_out is not None:
            kw['accum_out'] = accum_out
        return self.s.add('act', lambda e: e.activation(out, in_, func, **kw), reads, writes)

    def tt(self, eng, out, in0, in1, op, reads, writes):
        return self.s.add(eng, lambda e: e.tensor_tensor(out, in0, in1, op), reads, writes)

    def ts(self, eng, out, in0, s1, s2, op0, op1, reads, writes):
        if op1 is None:
            return self.s.add(eng, lambda e: e.tensor_scalar(out, in0, s1, None, op0), reads, writes)
        return self.s.add(eng, lambda e: e.tensor_scalar(out, in0, s1, s2, op0, op1), reads, writes)

    def stt(self, eng, out, in0, scalar, in1, op0, op1, reads, writes):
        return self.s.add(eng, lambda e: e.scalar_tensor_tensor(out, in0, scalar, in1, op0, op1), reads, writes)

    def copy(self, eng, out, in_, reads, writes):
        if eng == 'act':
            return self.s.add('act', lambda e: e.copy(out, in_), reads, writes)
        return self.s.add(eng, lambda e: e.tensor_copy(out, in_), reads, writes)

    def memset(self, eng, ap, val, writes):
        return self.s.add(eng, lambda e: e.memset(ap, val), (), writes)

    def build(self):
        from contextlib import ExitStack
        nc = self.nc
        with ExitStack() as stack:
            self.stack = stack
            self.declare()
            self.program()
            sems = {e: stack.enter_context(nc.semaphore("sem_" + e)) for e in ['pe', 'act', 'dve', 'pool']}
            block = stack.enter_context(nc.Block())
            self.s.emit(nc, block, sems, self.slots)
        return nc

    def declare(self):
        nc = self.nc
        i = self.din
        self.xp = i("xp", [TP, D])
        self.xs = i("xs", [TS, D])
        self.cc = i("cc", [NSEQ, D])
        self.ck = i("ck", [2, 4, 1024, D])
        self.cv = i("cv", [2, 4, 1024, D])
        self.sca = i("sca", [2, 4, 2, 1024])
        self.sfc = i("sfc", [4, 4, 2, DFF])
        self.w_mod = i("w_mod", [4, D, 6 * D])
        self.b_mod = i("b_mod", [4, 6 * D])
        self.norm_g = i("norm_g", [4, 4, D])
        self.w_in = i("w_in_ab", [2, D, 5120])
        self.w_conv_a = i("w_conv_a", [2, 3, 1024])
        self.g_sgu = i("g_sgu", [2, 1024])
        self.w_sgu = i("w_sgu", [2, 8, 128, 128])
        self.b_sgu = i("b_sgu", [2, 8, 128])
        self.w_out = i("w_out_ab", [2, D, D])
        self.w_qkv = i("w_qkv_sb", [2, D, 3 * D])
        self.w_o = i("w_o_sb", [2, D, D])
        self.w_up = i("w_ffn_up", [4, D, 2 * DFF])
        self.w_fconv = i("w_ffn_conv", [4, 3, DFF])
        self.b_fconv = i("b_ffn_conv", [4, DFF])
        self.w_down = i("w_ffn_down", [4, DFF, D])
        o = self.dout
        self.yp = o("yp", [TP, D])
        self.ys = o("ys", [TS, D])
        self.kp = o("kp", [2, TP, D])
        self.vp = o("vp", [2, TP, D])
        self.cap = o("cap", [2, 2, 1024])
        self.fcp = o("fcp", [4, 2, DFF])
        self.ks = o("ks", [2, TS, D])
        self.vs = o("vs", [2, TS, D])
        self.cas = o("cas", [2, 4, 2, 1024])
        self.fcs = o("fcs", [4, 4, 2, DFF])
        self.sgv = o("sgv", [2, TS, 1024])
        self.X = nc.dram_tensor("Xs", [16, 128, T], F32).ap()
        self.Y = nc.dram_tensor("Ys", [16, 128, T], F32).ap()
        self.QS = nc.dram_tensor("QSs", [16, 128, T], BF16).ap()
        self.KS = nc.dram_tensor("KSs", [16, 128, T], BF16).ap()
        if self.debug:
            self.dbgH = nc.dram_tensor("dbgH", [16, 128, T], BF16, kind="ExternalOutput").ap()
            self.dbgX = nc.dram_tensor("dbgX", [16, 128, T], F32, kind="ExternalOutput").ap()
            self.dbgM = nc.dram_tensor("dbgM", [128, 6 * 16 * NSEQ], F32, kind="ExternalOutput").ap()
        self.H = self.sb("H", [128, 16 * T], BF16)
        self.H3 = self.H[:, :].rearrange("p (c t) -> p c t", c=16)
        self.R = self.sb("R", [128, 47872], BF16)
        self.WS = self.sb("WS", [128, 4 * 2816], BF16)
        self.TMP = self.sb("TMP", [128, 4096], F32)
        self.ps = [self.stack.enter_context(nc.psum_tensor("ps%d" % b, [128, 512], F32)) for b in range(8)]
        self.ones_bf = self.sb("ones_bf", [128, 128], BF16)
        self.ident_f = self.sb("ident_f", [128, 128], F32)
        self.ident_b = self.sb("ident_b", [128, 128], BF16)
        self.ustrict = self.sb("ustrict", [128, 128], BF16)
        self.modT = self.sb("modT", [128, 96 * NSEQ], F32)
        self.epsD = self.sb("epsD", [128, 4], F32)
        self.gT = self.sb("gT", [128, 16 * 16], F32)
        self.vecs2 = [self.sb("vecs%d" % k, [128, 6 * 16 * NSEQ], F32) for k in range(2)]
        self.MODS = nc.dram_tensor("MODSs", [4, 128, 96 * NSEQ], F32).ap()
        Rf = self.R.bitcast(F32)
        self.iota_t = Rf[:, 0:512]
        self.bmT = Rf[:, 512:512 + 384]
        self.cT = Rf[:, 1024:1024 + 80]
        self.scT = self.R[:, 4096:4096 + 80]
        self.wl = [self.slot("wl%d" % k) for k in range(4)]
        self.sp_ld = [self.slot("ld%d" % k) for k in range(6)]
        self.sp_st = [self.slot("st%d" % k) for k in range(6)]
        self.misc = [self.slot("mi%d" % k) for k in range(4)]
        self.pl = [self.slot("pl%d" % k) for k in range(4)]

    def ws_view(self, sub, kc):
        off = sub * 2816
        return self.WS[:, off:off + kc * 128].rearrange("p (k n) -> p k n", k=kc)

    def program(self):
        self.setup_consts()
        self.mod_all()
        self.prologue()
        for l in range(self.nl):
            if l % 2 == 0:
                self.mixer_even(l)
            else:
                self.mixer_odd(l)
            self.update_pass(l, 0)
            self.ffn(l)
            self.update_pass(l, 1)
        if self.debug:
            self.s.fence()
            self.dma('sp', self.misc[0], self.dbgH.rearrange("c p t -> p c t"), self.H3, (), ())
            self.dma('sp', self.misc[1], self.dbgM, self.cur_vecs[:, :], (), ())
            self.dma('sp', self.misc[2], self.dbgX, self.X, (), ())

    def setup_consts(self):
        s = self.s
        K = 'const'
        self.memset('pool', self.ones_bf[:, :], 1.0, [K])
        self.memset('pool', self.epsD[:, 0:1], float(D * EPS), [K])
        self.memset('pool', self.epsD[:, 1:2], float(1024 * EPS), [K])
        def asel(out, in_, pattern, op, base, cm, rk, wk):
            return self.s.add('pool', lambda e: e.affine_select(out, in_, pattern, op, 0.0, base=base,
                                                                channel_multiplier=cm), rk, wk)
        self.asel = asel
        asel(self.ident_f[:, :], self.ones_bf[:, :], [[1, 128]], ALU.is_equal, 0, -1, [K], ['id'])
        asel(self.ident_b[:, :], self.ones_bf[:, :], [[1, 128]], ALU.is_equal, 0, -1, [K], ['id'])
        asel(self.ustrict[:, :], self.ones_bf[:, :], [[-1, 128]], ALU.is_gt, 0, 1, [K], ['id'])
        for a16 in range(16):
            self.dma('sp', self.misc[0], self.gT[:, a16 * 16:(a16 + 1) * 16],
                     self.norm_g[a16 // 4, a16 % 4, :].rearrange("(c p) -> p c", p=128), (), ['gT'], nc_ok=True)
        for l in range(4):
            self.dma('sp', self.misc[1], self.bmT[:, l * 96:(l + 1) * 96],
                     self.b_mod[l, :].rearrange("(n p) -> p n", p=128), (), ['bmT'], nc_ok=True)
        cT3 = self.cT.rearrange("p (c q) -> p c q", q=NSEQ)
        for q in range(NSEQ):
            self.dma('sp', self.misc[2], cT3[:, :, q], self.cc[q, :].rearrange("(c p) -> p c", p=128), (), ['cT'],
                     nc_ok=True)
        self.act(self.scT, self.cT, AF.Silu, ['cT'], ['scT'])

    def mod_all(self):
        cnt = 0
        scT3 = self.scT.rearrange("p (c q) -> p c q", q=NSEQ)
        for l in range(min(self.nl + 1, 4)):
            for n2 in range(48):
                big = cnt % 2
                cnt += 1
                wv = self.WS[:, big * 5632:big * 5632 + 4096].rearrange("p (k n) -> p k n", k=16)
                src_ = self.w_mod[l, :, n2 * 256:(n2 + 1) * 256].rearrange("(k p) n -> p k n", p=128)
                wk = ('ws', big)
                self.dma('pool', self.wl[big], wv, src_, (), [wk])
                for j in range(2):
                    n = n2 * 2 + j
                    b = self.bank()
                    pk = ('ps', b)
                    for k in range(16):
                        self.mm(self.ps[b][:, 0:NSEQ], wv[:, k, j * 128:(j + 1) * 128], scT3[:, k, :],
                                k == 0, k == 15, [wk, 'scT'], [pk])
                    o = n * NSEQ
                    self.act(self.modT[:, o:o + NSEQ], self.ps[b][:, 0:NSEQ], AF.Identity, [pk, 'bmT'],
                             ['modT'], bias=self.bmT[:, l * 96 + n:l * 96 + n + 1])
            self.dma('sp', self.misc[3], self.MODS[l], self.modT[:, :], ['modT'], [('MODS', l)])
        self.s.fence()

    def layer_vecs(self, l):
        sq = float(np.sqrt(D))
        self.dma('sp', self.misc[3], self.modT[:, :], self.MODS[l], [('MODS', l)], ['modT'])
        m = self.modT[:, :].rearrange("p (j c q) -> p j c q", j=6, q=NSEQ)
        v = self.vecs2[l % 2][:, :].rearrange("p (j c q) -> p j c q", j=6, q=NSEQ)
        g = self.gT[:, l * 64:(l + 1) * 64].rearrange("p (j c) -> p j c", j=4)
        rk = ['modT', 'gT']
        wk = [('vecs',)]
        self.cur_vecs = self.vecs2[l % 2]

        def gb(j):
            return g[:, j, :].unsqueeze(2).broadcast_to([128, 16, NSEQ])
        self.stt('dve', v[:, 0], m[:, 1], 1.0, gb(0), ALU.add, ALU.mult, rk, wk)
        self.ts('dve', v[:, 0], v[:, 0], sq, None, ALU.mult, None, wk, wk)
        self.copy('dve', v[:, 1], m[:, 0], rk, wk)
        self.stt('dve', v[:, 2], m[:, 2], sq, gb(1), ALU.mult, ALU.mult, rk, wk)
        self.stt('dve', v[:, 3], m[:, 4], 1.0, gb(2), ALU.add, ALU.mult, rk, wk)
        self.ts('dve', v[:, 3], v[:, 3], sq, None, ALU.mult, None, wk, wk)
        self.copy('dve', v[:, 4], m[:, 3], rk, wk)
        self.stt('dve', v[:, 5], m[:, 5], sq, gb(3), ALU.mult, ALU.mult, rk, wk)
        self.v4 = v

    def stats_rstd(self, src3, W, rk, tag, wsq, wrs):
        sqv, sqk = wsq
        rs, rsk = wrs
        self.act(sqv, src3, AF.Square, rk, [sqk])
        b = self.bank()
        pk = ('ps', b)
        for c in range(16):
            self.mm(self.ps[b][:, 0:W], self.ones_bf[:, :], sqv[:, c, :], c == 0, c == 15, [sqk, 'const'], [pk])
        self.act(rs, self.ps[b][:, 0:W], AF.Sqrt, [pk], [rsk], bias=self.epsD[:, 0:1])
        self.s.add('dve', lambda e: e.reciprocal(rs, rs), [rsk], [rsk])

    def seq_cols(self, col0, W):
        if col0 < TP:
            return [(0, 0, W)]
        out = []
        for sidx in range(4):
            out.append((1 + sidx, sidx * 32, 32))
        return out

    def make_h(self, xt3, xk, col0, W, ja, jb, ws):
        sqv, sqk, rs, rsk, tmp, tmpk = ws
        self.stats_rstd(xt3, W, [xk], 'h', (sqv, sqk), (rs, rsk))
        self.tt('dve', tmp, xt3, rs.unsqueeze(1).broadcast_to([128, 16, W]), ALU.mult, [xk, rsk], [tmpk])
        v = self.v4
        for c in range(16):
            for (q, lo, w) in self.seq_cols(col0, W):
                eng = 'act' if c % 2 == 0 else 'pool'
                hk = ('H', c, col0)
                out = self.H3[:, c, col0 + lo:col0 + lo + w]
                if eng == 'act':
                    self.act(out, tmp[:, c, lo:lo + w], AF.Identity, [tmpk, ('vecs',)], [hk],
                             bias=v[:, jb, c, q:q + 1], scale=v[:, ja, c, q:q + 1])
                else:
                    self.ts('pool', out, tmp[:, c, lo:lo + w], v[:, ja, c, q:q + 1], v[:, jb, c, q:q + 1],
                            ALU.mult, ALU.add, [tmpk, ('vecs',)], [hk])

    def upd_ws(self, W, par):
        Rf = self.R.bitcast(F32)
        base = par * 11264
        x3 = Rf[:, base:base + 16 * W].rearrange("p (c w) -> p c w", c=16)
        y3 = Rf[:, base + 4096:base + 4096 + 16 * W].rearrange("p (c w) -> p c w", c=16)
        rs = Rf[:, base + 8192:base + 8192 + W]
        rs2 = Rf[:, base + 8448:base + 8448 + W]
        sqb = self.R[:, 2 * base + 17920:2 * base + 17920 + 16 * W].rearrange("p (c w) -> p c w", c=16)
        return x3, y3, rs, rs2, sqb

    def prologue(self):
        self.layer_vecs(0)
        Rf = self.R.bitcast(F32)
        n = 0
        for j in range(17):
            par = n % 2
            n += 1
            col0 = j * 128
            x3, y3, rs, rs2, sqb = self.upd_ws(128, par)
            xtm = Rf[:, par * 11264 + 4096:par * 11264 + 4096 + 2048]
            tk = ('xtm', par)
            src = self.xp[col0:col0 + 128, :] if j < 16 else self.xs[:, :]
            self.dma('sp', self.sp_ld[par], xtm, src, (), [tk])
            xk = ('xt', par)
            for g4 in range(4):
                b = self.bank()
                pk = ('ps', b)
                for cc in range(4):
                    c = g4 * 4 + cc
                    self.tr(self.ps[b][:, cc * 128:(cc + 1) * 128], xtm[:, c * 128:(c + 1) * 128],
                            self.ident_f[:, :], [tk, 'id'], [pk])
                eng = 'dve' if g4 % 2 == 0 else 'act'
                self.copy(eng, x3[:, g4 * 4:(g4 + 1) * 4, :],
                          self.ps[b][:, :].rearrange("p (c w) -> p c w", c=4), [pk], [xk])
            self.dma('sp', self.sp_st[par], self.X.rearrange("c p t -> p c t")[:, :, col0:col0 + 128], x3,
                     [xk], [('X', col0)])
            tmp = y3
            self.make_h(x3, xk, col0, 128, 0, 1, (sqb, ('sq', par), rs, ('rs', par), tmp, tk))
        self.s.fence()

    def linear(self, wsrc, nchunks, kc, rhs_fn, tiles, epi):
        for j in range(nchunks):
            sub = self.wcnt % 4
            self.wcnt += 1
            wv = self.ws_view(sub, kc)
            wk = ('ws', sub)
            self.dma('pool', self.wl[sub], wv, wsrc(j).rearrange("(k p) n -> p k n", p=128), (), [wk])
            for (col0, W) in tiles:
                b = self.bank()
                pk = ('ps', b)
                for k in range(kc):
                    self.mm(self.ps[b][:, 0:W], wv[:, k, :], rhs_fn(k, col0, W), k == 0, k == kc - 1, [wk], [pk])
                epi(j, col0, W, b, pk)

    def y_store_epi(self, accumulate=False):
        st = {'n': 0}
        Tm = self.TMP

        def epi(j, col0, W, b, pk):
            par = st['n'] % 2
            st['n'] += 1
            stg = Tm[:, par * 512:par * 512 + W]
            sk = ('ystg', par)
            ydst = self.Y[j, :, col0:col0 + W]
            if accumulate:
                prev = Tm[:, 1024 + par * 512:1024 + par * 512 + W]
                pvk = ('yprev', par)
                self.dma('sp', self.sp_ld[par], prev, ydst, [('Y', j, col0)], [pvk])
                self.tt('dve', stg, self.ps[b][:, 0:W], prev, ALU.add, [pk, pvk], [sk])
            else:
                self.copy('act' if par == 0 else 'dve', stg, self.ps[b][:, 0:W], [pk], [sk])
            self.dma('sp', self.sp_st[par], ydst, stg, [sk], [('Y', j, col0)])
        return epi

    def update_pass(self, l, which):
        final = (l == self.nl - 1 and which == 1)
        v = self.v4
        jg = 2 if which == 0 else 5
        if which == 1 and not final:
            self.layer_vecs(l + 1)
            ja, jb, vn = 0, 1, self.v4
        else:
            ja, jb, vn = 3, 4, v
        tiles = [(c0, 256) for c0 in range(0, TP, 256)] + [(TP, 128)]
        for ti, (col0, W) in enumerate(tiles):
            par = ti % 2
            x3, y3, rs, rs2, sqb = self.upd_ws(W, par)
            xk, yk, sqk, rsk, rs2k = ('ux', par), ('uy', par), ('usq', par), ('urs', par), ('urs2', par)
            self.dma('sp', self.sp_ld[2 + par], x3, self.X.rearrange("c p t -> p c t")[:, :, col0:col0 + W], (), [xk])
            self.dma('sp', self.sp_ld[4 + par], y3, self.Y.rearrange("c p t -> p c t")[:, :, col0:col0 + W], (), [yk])
            self.stats_rstd(y3, W, [yk], 'u', (sqb, sqk), (rs, rsk))
            self.tt('dve', y3, y3, rs.unsqueeze(1).broadcast_to([128, 16, W]), ALU.mult, [yk, rsk], [yk])
            for c in range(16):
                for (q, lo, w) in self.seq_cols(col0, W):
                    eng = 'dve'
                    self.stt(eng, x3[:, c, lo:lo + w], y3[:, c, lo:lo + w], v[:, jg, c, q:q + 1], x3[:, c, lo:lo + w],
                             ALU.mult, ALU.add, [yk, xk, ('vecs',)], [xk])
            if not final:
                self.dma('sp', self.sp_st[2 + par], self.X.rearrange("c p t -> p c t")[:, :, col0:col0 + W], x3,
                         [xk], [('X', col0)])
                self.v4 = vn
                self.make_h(x3, xk, col0, W, ja, jb, (sqb, sqk, rs2, rs2k, y3, yk))
                self.v4 = v
            else:
                for sbk in range(W // 128):
                    ytm = y3.rearrange("p c w -> p (c w)")[:, sbk * 2048:(sbk + 1) * 2048] if W == 256 else \
                        y3.rearrange("p c w -> p (c w)")
                    for g4 in range(4):
                        b = self.bank()
                        pk = ('ps', b)
                        for cc in range(4):
                            c = g4 * 4 + cc
                            self.tr(self.ps[b][:, cc * 128:(cc + 1) * 128], x3[:, c, sbk * 128:(sbk + 1) * 128],
                                    self.ident_f[:, :], [xk, 'id'], [pk])
                        self.copy('dve' if g4 % 2 == 0 else 'act', ytm[:, g4 * 512:(g4 + 1) * 512], self.ps[b][:, :],
                                  [pk, yk], [yk])
                    if col0 < TP:
                        dst = self.yp[col0 + sbk * 128:col0 + (sbk + 1) * 128, :]
                    else:
                        dst = self.ys[:, :]
                    self.dma('sp', self.sp_st[2 + par], dst, ytm, [yk], [('yo', col0, sbk)])
        self.v4 = vn
        self.s.fence()

    def conv3(self, eng, Cb, Pb_views, wv, rk, wk):
        self.ts(eng, Cb, Pb_views[0], wv[0], None, ALU.mult, None, rk, wk)
        self.stt(eng, Cb, Pb_views[1], wv[1], Cb, ALU.mult, ALU.add, rk + wk, wk)
        self.stt(eng, Cb, Pb_views[2], wv[2], Cb, ALU.mult, ALU.add, rk + wk, wk)

    def mixer_even(self, l):
        i = l // 2
        R = self.R
        Rf = R.bitcast(F32)
        O3 = R[:, 0:16 * T].rearrange("p (c t) -> p c t", c=16)
        E0 = 16 * T // 2
        Tm = self.TMP
        s = self.s
        wvr = R[:, 8 * T:8 * T + 16384].rearrange("p (k n) -> p k n", k=16)
        for jj in range(4):
            self.dma('pool', self.wl[jj], wvr[:, :, jj * 256:(jj + 1) * 256],
                     self.w_in[i, :, 4096 + jj * 256:4096 + (jj + 1) * 256].rearrange("(k p) n -> p k n", p=128),
                     (), [('wvr', jj)])
        gbc = Rf[:, E0:E0 + 1024]
        svf = Rf[:, E0 + 1024:E0 + 2048]
        junk = R[:, 2 * (E0 + 2048):2 * (E0 + 2048) + 512]
        self.dma('sp', self.misc[0], gbc, self.g_sgu[i, :].partition_broadcast(128), (), ['gbc'])
        self.ts('dve', gbc, gbc, 32.0, None, ALU.mult, None, ['gbc'], ['gbc'])
        ssv = Tm[:, 0:4]
        for j in range(17):
            pks = []
            self.memset('pool', ssv[:, 0:2], 0.0, ['ssv'])
            bs_ = []
            for nh in range(2):
                b = self.bank()
                pk = ('ps', b)
                bs_.append((b, pk))
                for k in range(16):
                    self.mm(self.ps[b][:, :], self.H3[:, k, j * 128:(j + 1) * 128], wvr[:, k, nh * 512:(nh + 1) * 512],
                            k == 0, k == 15, [('wvr', 2 * nh), ('wvr', 2 * nh + 1)], [pk])
                self.act(junk, self.ps[b][:, :], AF.Square, [pk, 'ssv'], ['junk', 'ssv'], accum_out=ssv[:, nh:nh + 1])
            self.tt('dve', ssv[:, 2:3], ssv[:, 0:1], ssv[:, 1:2], ALU.add, ['ssv'], ['ssv2'])
            self.act(ssv[:, 3:4], ssv[:, 2:3], AF.Sqrt, ['ssv2'], ['ssv3'], bias=self.epsD[:, 1:2])
            s.add('dve', lambda e: e.reciprocal(ssv[:, 3:4], ssv[:, 3:4]), ['ssv3'], ['ssv3'])
            for nh in range(2):
                b, pk = bs_[nh]
                self.stt('dve', R[:, j * 1024 + nh * 512:j * 1024 + (nh + 1) * 512], self.ps[b][:, :], ssv[:, 3:4],
                         gbc[:, nh * 512:(nh + 1) * 512], ALU.mult, ALU.mult, [pk, 'ssv3', 'gbc'], [('vn', j)])
                if j == 16:
                    self.stt('dve', svf[:, nh * 512:(nh + 1) * 512], self.ps[b][:, :], ssv[:, 3:4],
                             gbc[:, nh * 512:(nh + 1) * 512], ALU.mult, ALU.mult, [pk, 'ssv3', 'gbc'], ['svf'])
        self.dma('sp', self.misc[1], self.sgv[i], svf, ['svf'], ['sgv'])
        s.fence()
        wstg = Rf[:, E0:E0 + 1024].rearrange("p (g j) -> p g j", g=8)
        sstg = Rf[:, E0 + 1024:E0 + 2048].rearrange("p (g j) -> p g j", g=8)
        wsT = R[:, 2 * (E0 + 2048):2 * (E0 + 2048) + 1024].rearrange("p (g i) -> p g i", g=8)
        wsS = R[:, 2 * (E0 + 2560):2 * (E0 + 2560) + 1024].rearrange("p (g i) -> p g i", g=8)
        brow = Rf[0:1, E0 + 3072:E0 + 4096]
        bhi = R[0:1, 2 * (E0 + 4096):2 * (E0 + 4096) + 1024]
        blo = R[0:1, 2 * (E0 + 4608):2 * (E0 + 4608) + 1024]
        bShi = R[0:1, 2 * (E0 + 5120):2 * (E0 + 5120) + 1024]
        bSlo = R[0:1, 2 * (E0 + 5632):2 * (E0 + 5632) + 1024]
        mxs = [Tm[:, 0:512], Tm[:, 512:1024]]
        self.dma('sp', self.misc[0], wstg, self.w_sgu[i].rearrange("g i j -> i g j"), (), ['wstg'])
        self.memset('pool', sstg, 0.0, ['sstg'])
        for sq_ in range(4):
            self.dma('sp', self.misc[1 + sq_ % 2], sstg[32 * sq_:32 * sq_ + 32, :, 32 * sq_:32 * sq_ + 32],
                     self.w_sgu[i, :, 0:32, 0:32].rearrange("g i j -> i g j"), ['sstg'], ['sstg'])
        for (stg, dst, kk) in ((wstg, wsT, 'wstg'), (sstg, wsS, 'sstg')):
            for g2 in range(2):
                b = self.bank()
                pk = ('ps', b)
                for gg in range(4):
                    g = g2 * 4 + gg
                    self.tr(self.ps[b][:, gg * 128:(gg + 1) * 128], stg[:, g, :], self.ident_f[:, :], [kk, 'id'], [pk])
                self.copy('dve', dst[:, g2 * 4:(g2 + 1) * 4, :], self.ps[b][:, :].rearrange("p (g i) -> p g i", g=4),
                          [pk], ['wsT'])
        self.memset('pool', wsT[64:128, :, 0:64], 0.0, ['wsT'])
        self.dma('sp', self.misc[3], brow, self.b_sgu[i].rearrange("g i -> (g i)").unsqueeze(0), (), ['brow'])
        self.copy('dve', bhi, brow, ['brow'], ['bhi'])
        self.tt('dve', blo, brow, bhi, ALU.subtract, ['brow', 'bhi'], ['blo'])
        for (src_, dst_) in ((bhi, bShi), (blo, bSlo)):
            self.copy('dve', dst_.rearrange("p (g s t) -> p g s t", g=8, s=4),
                      src_.rearrange("p (g i) -> p g i", g=8)[:, :, 0:32].unsqueeze(2).broadcast_to([1, 8, 4, 32]),
                      ['bhi', 'blo'], ['bS'])
        st = {'n': 0}
        ones_row = self.ones_bf[0:1, :]

        def epi_u(g, col0, W, b, pk):
            par = st['n'] % 2
            st['n'] += 1
            bm = self.bank()
            pm = ('ps', bm)
            for sbk in range(W // 128):
                j = col0 // 128 + sbk
                samp = (j == 16)
                ws_ = wsS if samp else wsT
                bh_, bl_ = (bShi, bSlo) if samp else (bhi, blo)
                o = self.ps[bm][:, sbk * 128:(sbk + 1) * 128]
                self.mm(o, R[:, j * 1024 + g * 128:j * 1024 + (g + 1) * 128], ws_[:, g, :], True, False,
                        [('vn', j), 'wsT'], [pm])
                self.mm(o, ones_row, bh_[:, g * 128:(g + 1) * 128], False, False, ['bhi', 'bS', 'const'], [pm])
                self.mm(o, ones_row, bl_[:, g * 128:(g + 1) * 128], False, True, ['blo', 'bS'], [pm])
            mk = ('mxs', par)
            self.copy('act', mxs[par][:, 0:W], self.ps[bm][:, 0:W], [pm], [mk])
            self.tt('dve', O3[:, 8 + g, col0:col0 + W], self.ps[b][:, 0:W], mxs[par][:, 0:W], ALU.mult, [pk, mk],
                    [('O', 8 + g, col0)])
        self.linear(lambda g: self.w_in[i, :, 3072 + g * 128:3072 + (g + 1) * 128], 8, 16,
                    lambda k, col0, W: self.H3[:, k, col0:col0 + W], TILES, epi_u)
        s.fence()
        wc = Tm[:, 3000:3024].rearrange("p (t c) -> p t c", t=3)
        for t in range(3):
            self.dma('sp', self.misc[t], wc[:, t, :], self.w_conv_a[i, t, :].rearrange("(c p) -> p c", p=128), (), ['wc'],
                     nc_ok=True)
        hist = Tm[:, 3024:3088].rearrange("p (c s t) -> p c s t", c=8, s=4)
        for c in range(8):
            for sq_ in range(4):
                self.dma('sp', self.misc[(c * 4 + sq_) % 4], hist[:, c, sq_, :],
                         self.sca[i, sq_, :, c * 128:(c + 1) * 128].rearrange("t p -> p t"), (), ['hist'], nc_ok=True)
        CS = Tm[:, 3088:3168].rearrange("p (c q t) -> p c q t", c=8, q=5)
        Pbs = [Rf[:, E0:E0 + 514], Rf[:, E0 + 514:E0 + 1028]]
        Cbs = [Rf[:, E0 + 1028:E0 + 1540], Rf[:, E0 + 1540:E0 + 2052]]
        xas = [Rf[:, E0 + 2052:E0 + 2564], Rf[:, E0 + 2564:E0 + 3076]]
        PS_ = Rf[:, E0 + 3076:E0 + 3076 + 136].rearrange("p (s t) -> p s t", s=4)
        st2 = {'n': 0, 'ps': {}}
        order = []
        for c in range(8):
            order += [c, 16 + c, 8 + c]

        def epi_a(jj, col0, W, b, pk):
            c = jj // 3
            kind = jj % 3
            st2['ps'][(kind, col0)] = (b, pk)
            return

        for c in range(8):
            wvs = []
            for kind, colidx in enumerate((c, 16 + c, 8 + c)):
                sub = self.wcnt % 4
                self.wcnt += 1
                wv = self.ws_view(sub, 16)
                wk = ('ws', sub)
                self.dma('pool', self.wl[sub], wv,
                         self.w_in[i, :, colidx * 128:(colidx + 1) * 128].rearrange("(k p) n -> p k n", p=128), (), [wk])
                wvs.append((wv, wk))
            for ti, (col0, W) in enumerate(TILES):
                par = st2['n'] % 2
                st2['n'] += 1
                bks = []
                for (wv, wk) in wvs:
                    b = self.bank()
                    pk = ('ps', b)
                    for k in range(16):
                        self.mm(self.ps[b][:, 0:W], wv[:, k, :], self.H3[:, k, col0:col0 + W], k == 0, k == 15, [wk], [pk])
                    bks.append((b, pk))
                (bx, pkx), (bc_, pkc), (bg, pkg) = bks
                xa = xas[par]
                xk_ = ('xa', par)
                Pb = Pbs[par]
                pbk = ('Pb', par)
                Cb = Cbs[par]
                cbk = ('Cb', par)
                self.copy('act', xa[:, 0:W], self.ps[bx][:, 0:W], [pkx], [xk_])
                wv3 = [wc[:, t, c:c + 1] for t in range(3)]
                if col0 < TP:
                    if ti == 0:
                        self.memset('pool', Pb[:, 0:2], 0.0, [pbk])
                    else:
                        self.copy('pool', Pb[:, 0:2], Pbs[1 - par][:, 512:514], [('Pb', 1 - par)], [pbk])
                    self.tt('dve', Pb[:, 2:2 + W], self.ps[bc_][:, 0:W], xa[:, 0:W], ALU.mult, [pkc, xk_], [pbk])
                    self.conv3('dve', Cb[:, 0:W], [Pb[:, t:t + W] for t in range(3)], wv3, [pbk, 'wc'], [cbk])
                    self.tt('dve', O3[:, c, col0:col0 + W], self.ps[bg][:, 0:W], Cb[:, 0:W], ALU.mult, [pkg, cbk],
                            [('O', c, col0)])
                    if col0 + W == TP:
                        self.copy('pool', CS[:, c, 0, :], Pb[:, W:W + 2], [pbk], ['CS'])
                else:
                    v4_ = lambda ap: ap.rearrange("p (s t) -> p s t", s=4)
                    self.copy('pool', PS_[:, :, 0:2], hist[:, c, :, :], ['hist'], ['PS_'])
                    self.tt('dve', PS_[:, :, 2:34], v4_(self.ps[bc_][:, 0:128]), v4_(xa[:, 0:128]), ALU.mult,
                            [pkc, xk_], ['PS_'])
                    Cs = v4_(Cb[:, 0:128])
                    self.conv3('dve', Cs, [PS_[:, :, t:t + 32] for t in range(3)], wv3, ['PS_', 'wc'], [cbk])
                    self.tt('dve', v4_(O3[:, c, TP:T]), v4_(self.ps[bg][:, 0:128]), Cs, ALU.mult, [pkg, cbk],
                            [('O', c, col0)])
                    self.copy('pool', CS[:, c, 1:5, :], PS_[:, :, 32:34], ['PS_'], ['CS'])
        for c in range(8):
            self.dma('sp', self.misc[c % 4], self.cap[i][:, c * 128:(c + 1) * 128].rearrange("t p -> p t"),
                     CS[:, c, 0, :], ['CS'], [('cap', c)], nc_ok=True)
            for sq_ in range(4):
                self.dma('sp', self.misc[(c + sq_) % 4],
                         self.cas[i, sq_][:, c * 128:(c + 1) * 128].rearrange("t p -> p t"),
                         CS[:, c, 1 + sq_, :], ['CS'], [('cas', c, sq_)], nc_ok=True)
        s.fence()
        self.linear(lambda n: self.w_out[i, :, n * 128:(n + 1) * 128], 16, 16,
                    lambda k, col0, W: O3[:, k, col0:col0 + W], TILES, self.y_store_epi())
        s.fence()

    def ffn(self, l):
        R = self.R
        Tm = self.TMP
        s = self.s
        A3 = R[:, 0:22 * T].rearrange("p (f t) -> p f t", f=22)
        wcv = Tm[:, 2400:2532].rearrange("p (t f) -> p t f", t=3)
        bcv = Tm[:, 2532:2576]
        FS = Tm[:, 2576:3016].rearrange("p (f q t) -> p f q t", f=44, q=5)
        hs = Tm[:, 3016:3368].rearrange("p (f s t) -> p f s t", f=44, s=4)
        for t in range(3):
            self.dma('sp', self.misc[t], wcv[:, t, :], self.w_fconv[l, t, :].rearrange("(f p) -> p f", p=128), (), ['wcv'],
                     nc_ok=True)
        self.dma('sp', self.misc[3], bcv, self.b_fconv[l, :].rearrange("(f p) -> p f", p=128), (), ['bcv'], nc_ok=True)
        for sq_ in range(4):
            for t in range(2):
                self.dma('sp', self.misc[(sq_ * 2 + t) % 4], hs[:, :, sq_, t],
                         self.sfc[l, sq_, t, :].rearrange("(f p) -> p f", p=128), (), ['hs'], nc_ok=True)
        Abs_ = [Tm[:, 0:514], Tm[:, 514:1028]]
        Cbs = [Tm[:, 1028:1540], Tm[:, 1540:2052]]
        AS_ = Tm[:, 2052:2188].rearrange("p (s t) -> p s t", s=4)
        for hf in range(2):
            st = {'n': 0, 'a': None}

            def epi_up(jj, col0, W, b, pk, hf=hf, st=st):
                ff = jj // 2
                f = hf * 22 + ff
                if jj % 2 == 0:
                    st['a'] = (b, pk)
                    return
                ba, pka = st['a']
                par = st['n'] % 2
                st['n'] += 1
                Ab = Abs_[par]
                abk = ('Ab', par)
                Cb = Cbs[par]
                cbk = ('Cb', par)
                wv3 = [wcv[:, t, f:f + 1] for t in range(3)]
                if col0 < TP:
                    if col0 == 0:
                        self.memset('pool', Ab[:, 0:2], 0.0, [abk])
                    else:
                        self.copy('pool', Ab[:, 0:2], Abs_[1 - par][:, 512:514], [('Ab', 1 - par)], [abk])
                    self.copy('act', Ab[:, 2:2 + W], self.ps[ba][:, 0:W], [pka], [abk])
                    self.conv3('dve', Cb[:, 0:W], [Ab[:, t:t + W] for t in range(3)], wv3, [abk, 'wcv'], [cbk])
                    self.act(Cb[:, 0:W], Cb[:, 0:W], AF.Silu, [cbk, 'bcv'], [cbk], bias=bcv[:, f:f + 1])
                    self.tt('dve', A3[:, ff, col0:col0 + W], Cb[:, 0:W], self.ps[b][:, 0:W], ALU.mult, [cbk, pk],
                            [('A', ff, col0)])
                    if col0 + W == TP:
                        self.copy('pool', FS[:, f, 0, :], Ab[:, W:W + 2], [abk], ['FS'])
                else:
                    v4_ = lambda ap: ap.rearrange("p (s t) -> p s t", s=4)
                    self.copy('pool', AS_[:, :, 0:2], hs[:, f, :, :], ['hs'], ['AS_'])
                    self.copy('act', AS_[:, :, 2:34], v4_(self.ps[ba][:, 0:128]), [pka], ['AS_'])
                    Cs = v4_(Cb[:, 0:128])
                    self.conv3('dve', Cs, [AS_[:, :, t:t + 32] for t in range(3)], wv3, ['AS_', 'wcv'], [cbk])
                    self.act(Cb[:, 0:128], Cb[:, 0:128], AF.Silu, [cbk, 'bcv'], [cbk], bias=bcv[:, f:f + 1])
                    self.tt('dve', A3[:, ff, TP:T], Cb[:, 0:128], self.ps[b][:, 0:128], ALU.mult, [cbk, pk],
                            [('A', ff, col0)])
                    self.copy('pool', FS[:, f, 1:5, :], AS_[:, :, 32:34], ['AS_'], ['FS'])

            def wsrc_up(jj, hf=hf):
                f = hf * 22 + jj // 2
                off = f * 128 + (DFF if jj % 2 == 1 else 0)
                return self.w_up[l, :, off:off + 128]
            for ff in range(22):
                wvs = []
                for jj in (2 * ff, 2 * ff + 1):
                    sub = self.wcnt % 4
                    self.wcnt += 1
                    wv = self.ws_view(sub, 16)
                    wk = ('ws', sub)
                    self.dma('pool', self.wl[sub], wv, wsrc_up(jj).rearrange("(k p) n -> p k n", p=128), (), [wk])
                    wvs.append((wv, wk))
                for (col0, W) in TILES:
                    for jj, (wv, wk) in zip((2 * ff, 2 * ff + 1), wvs):
                        b = self.bank()
                        pk = ('ps', b)
                        for k in range(16):
                            self.mm(self.ps[b][:, 0:W], wv[:, k, :], self.H3[:, k, col0:col0 + W], k == 0, k == 15,
                                    [wk], [pk])
                        epi_up(jj, col0, W, b, pk)
            s.fence()
            self.linear(lambda n, hf=hf: self.w_down[l, hf * 2816:(hf + 1) * 2816, n * 128:(n + 1) * 128], 16, 22,
                        lambda k, col0, W: A3[:, k, col0:col0 + W], TILES, self.y_store_epi(accumulate=(hf == 1)))
            s.fence()
        for t in range(2):
            self.dma('sp', self.misc[t], self.fcp[l, t, :].rearrange("(f p) -> p f", p=128), FS[:, :, 0, t], ['FS'],
                     [('fcp', t)], nc_ok=True)
            for sq_ in range(4):
                self.dma('sp', self.misc[(t + sq_) % 4], self.fcs[l, sq_, t, :].rearrange("(f p) -> p f", p=128),
                         FS[:, :, 1 + sq_, t], ['FS'], [('fcs', sq_, t)], nc_ok=True)
        s.fence()

    def mixer_odd(self, l):
        if self.odd_mode == 'skip':
            return
        i = l // 2
        R = self.R
        Tm = self.TMP
        Tb = Tm.bitcast(BF16)
        s = self.s
        O3 = R[:, 0:16 * T].rearrange("p (c t) -> p c t", c=16)
        st = {'n': 0}

        def epi_qkv(j, col0, W, b, pk):
            par = st['n'] % 2
            st['n'] += 1
            kind, h = j // 16, j % 16
            kf = Tm[:, 1024 + par * 512:1024 + par * 512 + W]
            kk = ('kf', par)
            if kind >= 1:
                self.copy('dve', kf, self.ps[b][:, 0:W], [pk], [kk])
            if kind < 2:
                stg = Tb[:, par * 512:par * 512 + W]
                sk = ('qstg', par)
                if kind == 0:
                    self.copy('act', stg, self.ps[b][:, 0:W], [pk], [sk])
                else:
                    self.copy('act', stg, kf, [kk], [sk])
                dst = (self.QS if kind == 0 else self.KS)[h, :, col0:col0 + W]
                self.dma('sp', self.sp_st[par], dst, stg, [sk], [('QK', kind, h, col0)])
            if kind >= 1:
                b2 = self.bank()
                pk2 = ('ps', b2)
                nsb = W // 128
                for sbk in range(nsb):
                    self.tr(self.ps[b2][:, sbk * 128:(sbk + 1) * 128], kf[:, sbk * 128:(sbk + 1) * 128],
                            self.ident_f[:, :], [kk, 'id'], [pk2])
                ktm = Tm[:, 2048 + par * 512:2048 + par * 512 + W]
                tk = ('ktm', par)
                self.copy('act' if par == 0 else 'dve', ktm, self.ps[b2][:, 0:W], [pk2], [tk])
                if col0 < TP:
                    dst = (self.kp if kind == 1 else self.vp)[i, col0:col0 + W, h * 128:(h + 1) * 128] \
                        .rearrange("(s p) d -> p s d", p=128)
                else:
                    dst = (self.ks if kind == 1 else self.vs)[i, :, h * 128:(h + 1) * 128].unsqueeze(1)
                self.dma('sp', self.sp_st[2 + par], dst, ktm.rearrange("p (s d) -> p s d", d=128), [tk],
                         [('KV', kind, h, col0)])
        self.linear(lambda j: self.w_qkv[i, :, j * 128:(j + 1) * 128], 48, 16,
                    lambda k, col0, W: self.H3[:, k, col0:col0 + W], TILES, epi_qkv)
        s.fence()
        if self.odd_mode == 'qkv':
            return
        Hh = self.H
        Hf = Hh.bitcast(F32)
        qTs = [Hh[:, p_ * 6400:p_ * 6400 + T] for p_ in range(2)]
        kTs = [Hh[:, p_ * 6400 + T:p_ * 6400 + 2 * T] for p_ in range(2)]
        Vts = [Hh[:, p_ * 6400 + 2 * T:p_ * 6400 + 2 * T + 2048].rearrange("p (b d) -> p b d", b=16) for p_ in range(2)]
        masks = Hh[:, 12800:14848].rearrange("p (r t) -> p r t", r=4)
        ones512 = Hh[:, 14848:15360]
        tb = 15360
        Rf_ = R.bitcast(F32)
        NPAR = 4

        def tmp_views(base_bf, buf_bf, buf_f):
            return (buf_f[:, base_bf // 2:base_bf // 2 + 512], buf_bf[:, base_bf + 1024:base_bf + 1536],
                    buf_bf[:, base_bf + 1536:base_bf + 2048],
                    buf_f[:, (base_bf + 2048) // 2:(base_bf + 2048) // 2 + 512], buf_bf[:, base_bf + 3072:base_bf + 3584])
        tv = [tmp_views(tb, Hh, Hf), tmp_views(tb + 3584, Hh, Hf),
              tmp_views(16 * T, R, Rf_), tmp_views(16 * T + 3584, R, Rf_)]
        es = [t_[0] for t_ in tv]
        sps = [t_[1] for t_ in tv]
        spms = [t_[2] for t_ in tv]
        tts = [t_[3] for t_ in tv]
        wss = [t_[4] for t_ in tv]
        Sb = Hh[:, 22528:23040]
        KCs = [Hh[:, 23040 + p_ * 1024:23040 + (p_ + 1) * 1024].rearrange("p (b d) -> p b d", b=8) for p_ in range(2)]
        KcT = [Hh[:, 25088 + q_ * 1024:25088 + (q_ + 1) * 1024] for q_ in range(4)]
        Vc = [Hh[:, 29184 + q_ * 1024:29184 + (q_ + 1) * 1024].rearrange("p (b d) -> p b d", b=8) for q_ in range(4)]
        Vsn = Hh[0:32, 33280:33792].rearrange("p (q d) -> p q d", q=4)
        self.memset('pool', ones512, 1.0, ['ones512'])
        self.memset('pool', self.epsD[:, 2:3], 1.0, ['one1'])
        for r in range(4):
            self.asel(masks[:, r, :], ones512, [[1, 512]], ALU.is_gt, -128 * r, -1, ['ones512'], ['masks'])
        sc = float(128 ** -0.5)
        one1 = self.epsD[:, 2:3]
        cnt = {'n': 0}

        Sbufs = [Sb, Hh[:, 33792:34304], Hh[:, 34304:34816], Tb[:, 0:512], Tb[:, 512:1024]]
        NS = len(Sbufs)
        LA = 3
        v3 = lambda ap: ap.rearrange("p (q t) -> p q t", q=4)

        def run_blocks(blks, ob, ok, use_start):
            n = len(blks)
            stt_ = [None] * n
            for sbuf in Sbufs:
                self.memset('pool', sbuf, 0.0, [('S', id(sbuf))])

            def A(k):
                bl = blks[k]
                P, c0, Wc = bl['P'], bl['c0'], bl['Wc']
                tp = cnt['n'] % NPAR
                cnt['n'] += 1
                zb = self.bank([2, 3, 4])
                zk = ('ps', zb)
                for (o_, l_, r_, rk_) in bl['zmm'](zb):
                    self.mm(o_, l_, r_, True, True, rk_, [zk])
                e, sp_, spm, tt_, w_ = es[tp], sps[tp], spms[tp], tts[tp], wss[tp]
                ek, spk, smk, tk_, wk_ = ('e', tp), ('sp', tp), ('spm', tp), ('tt', tp), ('w', tp)
                z = self.ps[zb][0:P, c0:c0 + Wc]
                self.act(e[0:P, c0:c0 + Wc], z, AF.Exp, [zk], [ek], scale=sc)
                self.act(sp_[0:P, c0:c0 + Wc], e[0:P, c0:c0 + Wc], AF.Ln, [ek, 'one1'], [spk], bias=one1[0:P, :])
                mk = bl['mask']
                vw = v3 if bl.get('m3') else (lambda ap: ap)
                if mk is not None:
                    self.tt('pool', vw(spm[0:P, c0:c0 + Wc]), vw(sp_[0:P, c0:c0 + Wc]), mk, ALU.mult,
                            [spk, 'masks'], [smk])
                    spm_v, smk_r = spm, smk
                else:
                    spm_v, smk_r = sp_, spk
                self.stt('dve', tt_[0:P, c0:c0 + Wc], z, sc, sp_[0:P, c0:c0 + Wc], ALU.mult, ALU.subtract,
                         [zk, spk], [tk_])
                if k + 1 < n:
                    Sn = Sbufs[(k + 1) % NS]
                    Sc = Sbufs[k % NS]
                    if k == 0:
                        self.copy('pool', Sn[0:P, c0:c0 + Wc], spm_v[0:P, c0:c0 + Wc], [smk_r], [('S', id(Sn))])
                    else:
                        self.tt('pool', Sn[0:P, c0:c0 + Wc], Sc[0:P, c0:c0 + Wc], spm_v[0:P, c0:c0 + Wc], ALU.add,
                                [smk_r, ('S', id(Sc))], [('S', id(Sn))])
                stt_[k] = (tp, spm_v, smk_r)

            def B(k):
                bl = blks[k]
                P, c0, Wc = bl['P'], bl['c0'], bl['Wc']
                tp, spm_v, smk_r = stt_[k]
                tt_, w_ = tts[tp], wss[tp]
                tk_, wk_ = ('tt', tp), ('w', tp)
                lb = self.bank([5, 6, 7])
                lk = ('ps', lb)
                lat = self.ps[lb][0:P, c0:c0 + Wc]
                self.mm(lat, self.ustrict[0:P, 0:P], spm_v[0:P, c0:c0 + Wc], True, k == 0, [smk_r, 'id'], [lk])
                if k > 0:
                    Sc = Sbufs[k % NS]
                    self.mm(lat, self.ones_bf[:, 0:P], Sc[:, c0:c0 + Wc], False, True, [('S', id(Sc)), 'const'], [lk])
                self.tt('dve', tt_[0:P, c0:c0 + Wc], tt_[0:P, c0:c0 + Wc], lat, ALU.subtract, [tk_, lk], [tk_])
                self.act(w_[0:P, c0:c0 + Wc], tt_[0:P, c0:c0 + Wc], AF.Exp, [tk_], [wk_])
                mk = bl['mask']
                vw = v3 if bl.get('m3') else (lambda ap: ap)
                if mk is not None:
                    self.tt('pool', vw(w_[0:P, c0:c0 + Wc]), vw(w_[0:P, c0:c0 + Wc]), mk, ALU.mult, [wk_, 'masks'], [wk_])

            def B2(k):
                bl = blks[k]
                P = bl['P']
                tp = stt_[k][0]
                w_, wk_ = wss[tp], ('w', tp)
                for (lhsT, lrk, oc0, wc0, ww) in bl['avs']:
                    self.mm(self.ps[ob][:, oc0:oc0 + ww], lhsT, w_[0:P, wc0:wc0 + ww], use_start and k == 0,
                            k == n - 1, [wk_] + lrk, [ok])
            for k in range(min(LA, n)):
                A(k)
            B(0)
            for k in range(n):
                if k + LA < n:
                    A(k + LA)
                if k + 1 < n:
                    B(k + 1)
                B2(k)

        for h in range(16):
            hp = h % 2
            qT, kT, Vt = qTs[hp], kTs[hp], Vts[hp]
            qk, kk_, vk = ('qT', hp), ('kT', hp), ('Vt', hp)
            self.dma('sp', self.sp_ld[hp], qT, self.QS[h], (), [qk])
            self.dma('sp', self.sp_ld[2 + hp], kT, self.KS[h], (), [kk_])
            self.dma('pool', self.wl[hp], Vt,
                     self.vp[i, :, h * 128:(h + 1) * 128].rearrange("(b p) d -> p b d", p=128), (), [vk])
            for qt in (range(4) if self.odd_mode != 'noprompt' else []):
                ob = self.bank([0, 1])
                ok = ('ps', ob)
                blks = []
                kmax = 4 * qt + 3
                for kb in range(kmax, -1, -1):
                    r = kb - 4 * qt
                    c0 = 128 * r if r > 0 else 0
                    Wc = 512 - c0

                    def zmm(zb, kb=kb, c0=c0, qt=qt):
                        return [(self.ps[zb][:, c0:512], kT[:, kb * 128:(kb + 1) * 128],
                                 qT[:, qt * 512 + c0:(qt + 1) * 512], [qk, kk_])]
                    blks.append(dict(zmm=zmm, P=128, c0=c0, Wc=Wc,
                                     mask=(masks[:, r, c0:512] if r >= 0 else None),
                                     avs=[(Vt[:, kb, :], [vk], c0, c0, Wc)]))
                run_blocks(blks, ob, ok, True)
                self.copy('act', O3[:, h, qt * 512:(qt + 1) * 512], self.ps[ob][:, :], [ok], [('O', h, qt)])
            if self.odd_mode == 'nosample':
                continue
            for q_ in range(4):
                kc_ = KCs[q_ % 2]
                kck = ('KC', q_ % 2)
                self.dma('pool', self.wl[2 + q_ % 2], kc_,
                         self.ck[i, q_, :, h * 128:(h + 1) * 128].rearrange("(b p) d -> p b d", p=128), (), [kck])
                tb_ = self.bank([2, 3, 4])
                tkk = ('ps', tb_)
                psb = self.ps[tb_].bitcast(BF16)
                for bb in range(8):
                    self.tr(psb[:, bb * 128:(bb + 1) * 128], kc_[:, bb, :], self.ident_b[:, :], [kck, 'id'], [tkk])
                self.copy('dve' if q_ % 2 == 0 else 'act', KcT[q_], psb[:, :], [tkk], [('KcT', q_)])
                self.dma('pool', self.pl[q_], Vc[q_],
                         self.cv[i, q_, :, h * 128:(h + 1) * 128].rearrange("(b p) d -> p b d", p=128), (), [('Vc', q_)])
                self.dma('pool', self.pl[q_], Vsn[:, q_, :], self.vs[i, 32 * q_:32 * q_ + 32, h * 128:(h + 1) * 128],
                         (), [('Vsn',)])
            ob = self.bank([0, 1])
            ok = ('ps', ob)
            self.memset('dve', self.ps[ob][:, 0:128], 0.0, [ok])
            blks = []

            def zmm_new(zb):
                return [(self.ps[zb][0:32, q_ * 32:(q_ + 1) * 32], kT[:, TP + 32 * q_:TP + 32 * q_ + 32],
                         qT[:, TP + 32 * q_:TP + 32 * q_ + 32], [qk, kk_]) for q_ in range(4)]
            mS = masks[0:32, 0, 0:32].unsqueeze(1).broadcast_to([32, 4, 32])
            blks.append(dict(zmm=zmm_new, P=32, c0=0, Wc=128, mask=mS, m3=True,
                             avs=[(Vsn[:, q_, :], [('Vsn',)], q_ * 32, q_ * 32, 32) for q_ in range(4)]))
            for kb in range(7, -1, -1):
                def zmm_c(zb, kb=kb):
                    return [(self.ps[zb][:, q_ * 32:(q_ + 1) * 32], KcT[q_][:, kb * 128:(kb + 1) * 128],
                             qT[:, TP + 32 * q_:TP + 32 * q_ + 32], [qk, ('KcT', q_)]) for q_ in range(4)]
                blks.append(dict(zmm=zmm_c, P=128, c0=0, Wc=128, mask=None,
                                 avs=[(Vc[q_][:, kb, :], [('Vc', q_)], q_ * 32, q_ * 32, 32) for q_ in range(4)]))
            run_blocks(blks, ob, ok, False)
            self.copy('act', O3[:, h, TP:T], self.ps[ob][:, 0:128], [ok], [('O', h, 4)])
        s.fence()
        self.linear(lambda n: self.w_o[i, :, n * 128:(n + 1) * 128], 16, 16,
                    lambda k, col0, W: O3[:, k, col0:col0 + W], TILES, self.y_store_epi())
        s.fence()


def make_in_maps(inp):
    f = lambda a: np.ascontiguousarray(np.asarray(a, dtype=np.float32))
    shared = {}
    for k in ['w_mod', 'b_mod', 'norm_g', 'w_in_ab', 'w_conv_a', 'g_sgu', 'w_sgu', 'b_sgu', 'w_out_ab',
              'w_qkv_sb', 'w_o_sb', 'w_ffn_up', 'w_ffn_conv', 'b_ffn_conv', 'w_ffn_down']:
        shared[k] = f(inp[k])
    maps = []
    for b in range(8):
        m = dict(shared)
        sl = slice(4 * b, 4 * b + 4)
        m['xp'] = f(inp['x_prompt'][b])
        m['xs'] = f(inp['x_sample'][sl]).reshape(TS, D)
        m['cc'] = f(np.concatenate([np.asarray(inp['c_prompt'])[b:b + 1], np.asarray(inp['c_sample'])[sl]], axis=0))
        m['ck'] = f(np.asarray(inp['cache_sb_k'])[:, sl]).reshape(2, 4, 1024, D)
        m['cv'] = f(np.asarray(inp['cache_sb_v'])[:, sl]).reshape(2, 4, 1024, D)
        m['sca'] = f(np.asarray(inp['state_conv_a'])[:, sl])
        m['sfc'] = f(np.asarray(inp['state_ffn_conv'])[:, sl])
        maps.append(m)
    return maps


_NC_CACHE = {}


def kernel(**inputs):
    if 'nc' not in _NC_CACHE:
        _NC_CACHE['nc'] = Builder().build()
    nc = _NC_CACHE['nc']
    maps = make_in_maps(inputs)
    res = run_bass_kernel_spmd(nc, maps, core_ids=list(range(8)))
    r = res.results
    cat = lambda k, ax: np.stack([np.asarray(x[k]) for x in r], axis=ax)
    y_prompt = cat('yp', 0)
    y_sample = cat('ys', 0).reshape(32, 32, D)
    k_p = cat('kp', 1).reshape(2, 8, TP, 16, 128)
    v_p = cat('vp', 1).reshape(2, 8, TP, 16, 128)
    ca_p = cat('cap', 1)
    f_p = cat('fcp', 1)
    k_s = cat('ks', 1).reshape(2, 32, 32, 16, 128)
    v_s = cat('vs', 1).reshape(2, 32, 32, 16, 128)
    ca_s = cat('cas', 1).reshape(2, 32, 2, 1024)
    f_s = cat('fcs', 1).reshape(4, 32, 2, DFF)
    sg = cat('sgv', 1).reshape(2, 32, 32, 1024)
    return (y_prompt, y_sample, k_p, v_p, ca_p, f_p, k_s, v_s, ca_s, f_s, sg)
```
